# Optimizing a Trainium2 kernel written in Bass

```python
import jax, jax.numpy as jnp
from jax import lax
import numpy as np


D_MODEL = 2048
BATCH = 8
SEQ = 2048
DEPTH = 1

GRID_W = 64
CTX_LEN = 256
HEAD_DIM = 128
MIX_WIDTH = D_MODEL
ATT_HEADS = MIX_WIDTH // 2 // HEAD_DIM
KV_HEADS = ATT_HEADS // 4
GROUP = ATT_HEADS // KV_HEADS
M_HEADS = (MIX_WIDTH - ATT_HEADS * HEAD_DIM) // HEAD_DIM
ATT_Q_W = ATT_HEADS * HEAD_DIM
KV_W = KV_HEADS * HEAD_DIM
M_W = M_HEADS * HEAD_DIM
N_GATES = 4 * M_HEADS
PROJ_WIDTH = ATT_Q_W + 2 * KV_W + 4 * M_W + N_GATES
SPLIT_POINTS = (ATT_Q_W, ATT_Q_W + KV_W, ATT_Q_W + 2 * KV_W, ATT_Q_W + 2 * KV_W + 2 * M_W,
                ATT_Q_W + 2 * KV_W + 3 * M_W, ATT_Q_W + 2 * KV_W + 4 * M_W)
AXIS_DIM = HEAD_DIM // 2
ROPE_THETA = 10000.0
Q_BLOCK = 128
MLSTM_CHUNK = 64
CONV_K = 5
D_FF = 256 * ((8 * D_MODEL // 3 + 255) // 256)
N_MOD = 9
RMS_EPS = 1e-6

kernel_name = 'hybrid_gqa_mlstm_macaron_dit_layer'


def rms_norm(x, g):
    xf = x.astype(jnp.float32)
    y = xf * lax.rsqrt(jnp.mean(xf * xf, axis=-1, keepdims=True) + RMS_EPS)
    return (y * g.astype(jnp.float32)).astype(x.dtype)


def adaln(x, g, mod, i):
    return rms_norm(x, g) * (1 + mod[:, i + 1, None, :]) + mod[:, i, None, :]


def half_step_ffn(x, mod, i, g, w_up, w_down):
    h = adaln(x, g, mod, i)
    a, b = jnp.split(h @ w_up, 2, axis=-1)
    return x + 0.5 * mod[:, i + 2, None, :] * ((jax.nn.silu(a) * b) @ w_down)


def axial_rope_tables(rows):
    row = jnp.repeat(jnp.arange(rows), GRID_W).astype(jnp.float32)
    col = jnp.tile(jnp.arange(GRID_W), rows).astype(jnp.float32)
    inv = jnp.power(ROPE_THETA, -jnp.arange(0, AXIS_DIM, 2, dtype=jnp.float32) / AXIS_DIM)
    ang_r = row[:, None] * inv[None, :]
    ang_c = col[:, None] * inv[None, :]
    return (jnp.cos(ang_r), jnp.sin(ang_r), jnp.cos(ang_c), jnp.sin(ang_c))


def rotate_half(x, cos, sin):
    x1, x2 = jnp.split(x, 2, axis=-1)
    c, s = cos[:, None, :], sin[:, None, :]
    return jnp.concatenate([x1 * c - x2 * s, x2 * c + x1 * s], axis=-1)


def apply_axial_rope(x, rope):
    cr, sr, cc, sc = rope
    xr, xcol = jnp.split(x.astype(jnp.float32), 2, axis=-1)
    return jnp.concatenate([rotate_half(xr, cr, sr), rotate_half(xcol, cc, sc)], axis=-1).astype(x.dtype)


def short_conv(x, w, b):
    y = lax.conv_general_dilated(x, w[:, None, :].astype(x.dtype), window_strides=(1,),
                                 padding=[(CONV_K // 2, CONV_K // 2)],
                                 dimension_numbers=('NWC', 'WIO', 'NWC'),
                                 feature_group_count=x.shape[-1])
    return y + b


def mixer_projections(h, w_in, q_gain, k_gain, conv_w, conv_b, gate_b, rope):
    B, T, _ = h.shape
    aq, ak, av, mqk, mv, mo, gt = jnp.split(h @ w_in, SPLIT_POINTS, axis=-1)
    aq = rms_norm(aq.reshape(B, T, ATT_HEADS, HEAD_DIM), q_gain)
    ak = rms_norm(ak.reshape(B, T, KV_HEADS, HEAD_DIM), k_gain)
    if rope is not None:
        aq = apply_axial_rope(aq, rope)
        ak = apply_axial_rope(ak, rope)
    av = av.reshape(B, T, KV_HEADS, HEAD_DIM)
    mq, mk = jnp.split(jax.nn.silu(short_conv(mqk, conv_w, conv_b)), 2, axis=-1)
    mq = mq.reshape(B, T, M_HEADS, HEAD_DIM)
    mk = mk.reshape(B, T, M_HEADS, HEAD_DIM) * HEAD_DIM ** -0.5
    mv = mv.reshape(B, T, M_HEADS, HEAD_DIM)
    mo = jax.nn.sigmoid(mo)
    gt = (gt.astype(jnp.float32) + gate_b.astype(jnp.float32)).reshape(B, T, 4, M_HEADS)
    ig_f = gt[:, :, 0]
    lf_f = jax.nn.log_sigmoid(gt[:, :, 1])
    ig_b = gt[:, :, 2]
    lf_b = jax.nn.log_sigmoid(gt[:, :, 3])
    return (aq, ak, av), (mq, mk, mv, mo, ig_f, lf_f, ig_b, lf_b)


def attend(q, k, v):
    s = jnp.einsum('bqhgd,bkhd->bhgqk', q, k, preferred_element_type=jnp.float32) * HEAD_DIM ** -0.5
    p = jax.nn.softmax(s, axis=-1).astype(v.dtype)
    return jnp.einsum('bhgqk,bkhd->bqhgd', p, v)


def latent_attention(q, k_all, v_all):
    B, T = q.shape[:2]
    qb = q.reshape(B, T // Q_BLOCK, Q_BLOCK, KV_HEADS, GROUP, HEAD_DIM).swapaxes(0, 1)
    out = lax.map(lambda blk: attend(blk, k_all, v_all), qb)
    return out.swapaxes(0, 1).reshape(B, T, ATT_Q_W)


def mlstm_chunk(carry, inp):
    C0, n0, m0 = carry
    q, k, v, ig, lf = inp
    L = q.shape[2]
    b = jnp.cumsum(lf, axis=-1)
    lower = jnp.tril(jnp.ones((L, L), dtype=bool))
    d_log = jnp.where(lower, b[..., :, None] - b[..., None, :] + ig[..., None, :], -jnp.inf)
    inter = b + m0[..., None]
    m = jnp.maximum(inter, jnp.max(d_log, axis=-1))
    s = jnp.einsum('bhld,bhsd->bhls', q, k) * jnp.exp(d_log - m[..., None])
    a = jnp.exp(inter - m)
    num = jnp.einsum('bhls,bhsd->bhld', s, v) + a[..., None] * jnp.einsum('bhld,bhde->bhle', q, C0)
    den = jnp.abs(jnp.sum(s, axis=-1) + a * jnp.einsum('bhld,bhd->bhl', q, n0))
    h = num / jnp.maximum(den, jnp.exp(-m))[..., None]
    g = b[..., -1:] - b + ig
    total = b[..., -1] + m0
    m_new = jnp.maximum(total, jnp.max(g, axis=-1))
    wg = jnp.exp(g - m_new[..., None])
    decay = jnp.exp(total - m_new)
    C_new = decay[..., None, None] * C0 + jnp.einsum('bhs,bhsd,bhse->bhde', wg, k, v)
    n_new = decay[..., None] * n0 + jnp.einsum('bhs,bhsd->bhd', wg, k)
    return (C_new, n_new, m_new), h


def mlstm_scan(q, k, v, ig, lf, state):
    B, T, H, d = q.shape
    nc = T // MLSTM_CHUNK

    def chunks(a):
        a = a.astype(jnp.float32).reshape((B, nc, MLSTM_CHUNK) + a.shape[2:])
        return jnp.moveaxis(jnp.moveaxis(a, 3, 2), 1, 0)

    state, h = lax.scan(mlstm_chunk, state, (chunks(q), chunks(k), chunks(v), chunks(ig), chunks(lf)))
    h = jnp.moveaxis(jnp.moveaxis(h, 0, 1), 2, 3).reshape(B, T, H, d)
    return h, state


def mlstm_bidir(q, k, v, ig_f, lf_f, ig_b, lf_b, s_f, s_b):
    h_f, s_f = mlstm_scan(q, k, v, ig_f, lf_f, s_f)
    flip = lambda a: jnp.flip(a, axis=1)
    h_b, s_b = mlstm_scan(flip(q), flip(k), flip(v), flip(ig_b), flip(lf_b), s_b)
    return (h_f + flip(h_b)).astype(q.dtype), s_f, s_b


def mixer_output(att, hm, mo, m_gain, w_out):
    B, T = att.shape[:2]
    hm = rms_norm(hm, m_gain.reshape(M_HEADS, HEAD_DIM)) * mo.reshape(B, T, M_HEADS, HEAD_DIM)
    return jnp.concatenate([att.reshape(B, T, ATT_Q_W), hm.reshape(B, T, M_W)], axis=-1) @ w_out


def setup_inputs(seed: int = 0) -> dict:
    key = jax.random.key(seed)
    ks = jax.random.split(key, 20)
    D = D_MODEL
    nrm = lambda k, shape, scale: jax.random.normal(k, shape, jnp.float32) * scale
    f_bias = jnp.linspace(3.0, 6.0, M_HEADS, dtype=jnp.float32)
    gate_sel = jnp.array([0.0, 1.0, 0.0, 1.0], jnp.float32)
    gate_b = (nrm(ks[16], (DEPTH, 4, M_HEADS), 0.1)
              + gate_sel[None, :, None] * f_bias[None, None, :]).reshape(DEPTH, N_GATES)
    return {
        'x': nrm(ks[0], (BATCH, SEQ, D), 1.0),
        'c': nrm(ks[1], (BATCH, D), 1.0),
        'ctx': nrm(ks[2], (BATCH, CTX_LEN, D), 1.0),
        'c_ctx': nrm(ks[3], (D,), 1.0),
        'w_mod': nrm(ks[4], (DEPTH, D, N_MOD * D), 0.5 * D ** -0.5),
        'b_mod': nrm(ks[5], (DEPTH, N_MOD * D), 0.02),
        'g_norm': 1.0 + nrm(ks[6], (DEPTH, 3, D), 0.02),
        'w_ffn1_up': nrm(ks[7], (DEPTH, D, 2 * D_FF), D ** -0.5),
        'w_ffn1_down': nrm(ks[8], (DEPTH, D_FF, D), D_FF ** -0.5),
        'w_ffn2_up': nrm(ks[9], (DEPTH, D, 2 * D_FF), D ** -0.5),
        'w_ffn2_down': nrm(ks[10], (DEPTH, D_FF, D), D_FF ** -0.5),
        'w_in': nrm(ks[11], (DEPTH, D, PROJ_WIDTH), D ** -0.5),
        'q_gain': 1.0 + nrm(ks[12], (DEPTH, HEAD_DIM), 0.02),
        'k_gain': 1.0 + nrm(ks[13], (DEPTH, HEAD_DIM), 0.02),
        'conv_w': nrm(ks[14], (DEPTH, CONV_K, 2 * M_W), CONV_K ** -0.5),
        'conv_b': nrm(ks[15], (DEPTH, 2 * M_W), 0.02),
        'gate_b': gate_b,
        'm_gain': 1.0 + nrm(ks[17], (DEPTH, M_W), 0.02),
        'w_out': nrm(ks[18], (DEPTH, MIX_WIDTH, D), MIX_WIDTH ** -0.5),
        'g_final': 1.0 + nrm(ks[19], (D,), 0.02),
    }


def reference(x, c, ctx, c_ctx, w_mod, b_mod, g_norm, w_ffn1_up, w_ffn1_down, w_ffn2_up, w_ffn2_down,
              w_in, q_gain, k_gain, conv_w, conv_b, gate_b, m_gain, w_out, g_final):
    B, T, _ = x.shape
    ROWS = T // GRID_W
    rope = axial_rope_tables(ROWS)
    silu_c = jax.nn.silu(c)
    silu_cc = jax.nn.silu(c_ctx)[None, :]
    zero_state = (jnp.zeros((B, M_HEADS, HEAD_DIM, HEAD_DIM), jnp.float32),
                  jnp.zeros((B, M_HEADS, HEAD_DIM), jnp.float32),
                  jnp.zeros((B, M_HEADS), jnp.float32))
    xl, xc = x, ctx
    for l in range(DEPTH):
        mod_l = (silu_c @ w_mod[l] + b_mod[l]).reshape(B, N_MOD, D_MODEL)
        mod_c = (silu_cc @ w_mod[l] + b_mod[l]).reshape(1, N_MOD, D_MODEL)
        xl = half_step_ffn(xl, mod_l, 0, g_norm[l, 0], w_ffn1_up[l], w_ffn1_down[l])
        xc = half_step_ffn(xc, mod_c, 0, g_norm[l, 0], w_ffn1_up[l], w_ffn1_down[l])
        hl = adaln(xl, g_norm[l, 1], mod_l, 3)
        hc = adaln(xc, g_norm[l, 1], mod_c, 3)
        mix = (w_in[l], q_gain[l], k_gain[l], conv_w[l], conv_b[l], gate_b[l])
        (aq_l, ak_l, av_l), m_l = mixer_projections(hl, *mix, rope)
        (aq_c, ak_c, av_c), m_c = mixer_projections(hc, *mix, None)
        hm_c, st_f, st_b = mlstm_bidir(*m_c[:3], *m_c[4:], zero_state, zero_state)
        hm_l, _, _ = mlstm_bidir(*m_l[:3], *m_l[4:], st_f, st_b)
        att_l = latent_attention(aq_l, jnp.concatenate([ak_c, ak_l], axis=1),
                                 jnp.concatenate([av_c, av_l], axis=1))
        xl = xl + mod_l[:, 5, None, :] * mixer_output(att_l, hm_l, m_l[3], m_gain[l], w_out[l])
        xl = half_step_ffn(xl, mod_l, 6, g_norm[l, 2], w_ffn2_up[l], w_ffn2_down[l])
        if l + 1 < DEPTH:
            Bc, Tc = aq_c.shape[:2]
            att_c = attend(aq_c.reshape(Bc, Tc, KV_HEADS, GROUP, HEAD_DIM), ak_c, av_c)
            xc = xc + mod_c[:, 5, None, :] * mixer_output(att_c, hm_c, m_c[3], m_gain[l], w_out[l])
            xc = half_step_ffn(xc, mod_c, 6, g_norm[l, 2], w_ffn2_up[l], w_ffn2_down[l])
    return rms_norm(xl, g_final)
```

```python
import contextlib
import numpy as np
import concourse.bass as bass
import concourse.mybir as mybir
from concourse.bass_utils import run_bass_kernel_spmd

F32 = mybir.dt.float32
BF16 = mybir.dt.bfloat16
AF = mybir.ActivationFunctionType
ALU = mybir.AluOpType

D = 2048
KD = 16
FF = 5632
KF = 44
NLAT = 2048
NCTX = 256
NTOK = NLAT + NCTX
NT = NTOK // 128
EPS = 1e-6
TILES = [(0, 256)] + [(256 + 512 * i, 512) for i in range(4)]
ENGS = ("pe", "act", "dve", "pool", "sp")


class Prog:
    def __init__(self, nc):
        self.nc = nc
        self.ops = {e: [] for e in ENGS}
        self.cnt = {e: 0 for e in ENGS}
        self.unsig = {e: False for e in ENGS}
        self.seen = {e: {} for e in ENGS}
        self.lastw = {}
        self.readers = {}
        self.dma_cnt = {}
        self.same_wait = {"pe": False, "act": True, "dve": True, "pool": True, "sp": False}

    def _deps(self, eng, reads, writes):
        deps = []
        for r in reads:
            ev = self.lastw.get(r)
            if ev is not None:
                deps.append((ev, "raw"))
        for w in writes:
            ev = self.lastw.get(w)
            if ev is not None:
                deps.append((ev, "waw"))
            for ev in self.readers.get(w, ()):
                deps.append((ev, "war"))
        waits = {}
        for (key, val), kind in deps:
            if key == eng and not self.same_wait[eng]:
                continue
            if self.seen[eng].get(key, 0) >= val:
                continue
            if waits.get(key, 0) < val:
                waits[key] = val
        for k, v in waits.items():
            self.seen[eng][k] = v
        return list(waits.items())

    def _record(self, ev, reads, writes):
        for r in reads:
            self.readers.setdefault(r, []).append(ev)
        for w in writes:
            self.lastw[w] = ev
            self.readers[w] = []

    def op(self, eng, fn, reads=(), writes=(), signal=True):
        waits = self._deps(eng, reads, writes)
        ev = (eng, self.cnt[eng] + 1)
        if signal:
            self.cnt[eng] += 1
            self.unsig[eng] = False
        else:
            self.unsig[eng] = True
        self.ops[eng].append((fn, waits, eng if signal else None, 1))
        self._record(ev, reads, writes)
        return ev

    def dma(self, queue, sem, fn, reads=(), writes=()):
        waits = self._deps(queue, reads, writes)
        self.dma_cnt[sem] = self.dma_cnt.get(sem, 0) + 16
        ev = (sem, self.dma_cnt[sem])
        self.ops[queue].append((fn, waits, sem, 16))
        self._record(ev, reads, writes)
        return ev

    def wait_all(self, eng, events):
        waits = []
        for key, val in events:
            if self.seen[eng].get(key, 0) < val:
                self.seen[eng][key] = val
                waits.append((key, val))
        self.ops[eng].append((None, waits, None, 0))

    def barrier(self):
        for e in ENGS:
            assert not self.unsig[e], e
        evs = [(e, self.cnt[e]) for e in ENGS if self.cnt[e] > 0]
        evs += [(s, c) for s, c in self.dma_cnt.items()]
        for e in ENGS:
            self.wait_all(e, [ev for ev in evs if ev[0] != e])
        self.lastw = {}
        self.readers = {}

    def emit(self):
        nc = self.nc
        for e in ENGS:
            assert not self.unsig[e], e
            assert self.cnt[e] < 60000, (e, self.cnt[e])
        semnames = list(ENGS) + sorted(self.dma_cnt.keys())
        with contextlib.ExitStack() as st:
            sems = {n: st.enter_context(nc.semaphore("s_" + n)) for n in semnames}
            block = st.enter_context(nc.Block())

            def run(handle, e):
                for fn, waits, sig, inc in self.ops[e]:
                    for k, v in waits:
                        handle.wait_ge(sems[k], v)
                    if fn is None:
                        continue
                    ins = fn(handle)
                    if sig is not None:
                        ins.then_inc(sems[sig], inc)

            @block.tensor
            def _(h):
                run(h, "pe")

            @block.scalar
            def _(h):
                run(h, "act")

            @block.vector
            def _(h):
                run(h, "dve")

            @block.gpsimd
            def _(h):
                run(h, "pool")

            @block.sync
            def _(h):
                run(h, "sp")


def build_nc(dbg=False, stop_after=None):
    nc = bass.Bass("TRN2", target_bir_lowering=False)
    din = lambda n, s, dt=F32: nc.dram_tensor(n, s, dt, kind="ExternalInput").ap()
    x_d = din("x", [NLAT, D])
    ctx_d = din("ctx", [NCTX, D])
    pa_d = din("pa", [96, 128])
    pb_d = din("pb", [106, 128])
    cst_d = din("cst", [128, 640])
    rope_d = din("rope", [128, 2 * NLAT])
    wmod_d = din("w_mod", [D, 9 * D])
    bmod_d = din("b_mod", [1, 9 * D])
    w1u_d = din("w_ffn1_up", [D, 2 * FF])
    w1d_d = din("w_ffn1_down", [FF, D])
    w2u_d = din("w_ffn2_up", [D, 2 * FF])
    w2d_d = din("w_ffn2_down", [FF, D])
    win_d = din("w_in", [D, 5664])
    wout_d = din("w_out", [D, D])
    gb_d = din("gate_b", [1, 32])
    out_d = nc.dram_tensor("out", [NLAT, D], F32, kind="ExternalOutput").ap()
    kind_s = "ExternalOutput" if dbg else "Internal"
    x1s = nc.dram_tensor("x1s", [KD, 128, NLAT], F32, kind=kind_s).ap()
    mixs = nc.dram_tensor("mixs", [KD, 128, NLAT], BF16, kind=kind_s).ap()
    if dbg:
        h2s = nc.dram_tensor("h2s", [KD, 128, NTOK], BF16, kind="ExternalOutput").ap()
        mods = nc.dram_tensor("mods", [128, 288], F32, kind="ExternalOutput").ap()

    st = contextlib.ExitStack()
    sb = lambda n, s, dt: st.enter_context(nc.sbuf_tensor(n, s, dt))
    CST = sb("CST", [128, 640], F32)
    PAT = sb("PAT", [128, 96], F32)
    PBT = sb("PBT", [128, 106], F32)
    MODT = sb("MODT", [128, 288], F32)
    PRM = sb("PRM", [128, 192], F32)
    CBF = sb("CBF", [128, 256], BF16)
    H2 = sb("H2", [128, KD * NTOK], BF16)
    WB = sb("WB", [128, 24576], BF16)
    A32 = sb("A32", [128, 12288], F32)
    A16 = sb("A16", [128, 14336], BF16)
    GT = sb("GT", [128, NT * 32], F32)
    PS = [st.enter_context(nc.psum_tensor(f"ps{i}", [128, 512], F32)) for i in range(8)]

    IDENT = CST[:, 0:128]
    MASKF = CST[:, 128:256]
    MASKB = CST[:, 256:384]
    ONES = CST[:, 384:512]
    RM = CST[:, 512:640]
    IDB = CBF[:, 0:128]
    ONB = CBF[:, 128:256]
    H2v = H2[:].rearrange("p (k n) -> p k n", k=KD)
    MODv = MODT[:].rearrange("p (i k r) -> p i k r", i=9, k=KD)
    WB32 = WB.bitcast(F32)

    P = Prog(nc)
    act = lambda fn, **kw: P.op("act", fn, **kw)
    dve = lambda fn, **kw: P.op("dve", fn, **kw)
    pool = lambda fn, **kw: P.op("pool", fn, **kw)
    pe = lambda fn, **kw: P.op("pe", fn, **kw)

    def a32(off, n, pat=None, **dims):
        v = A32[:, off:off + n]
        return v.rearrange(pat, **dims) if pat else v

    def a16(off, n, pat=None, **dims):
        v = A16[:, off:off + n]
        return v.rearrange(pat, **dims) if pat else v

    P.dma("sp", "c0", lambda h: h.dma_start(out=CST[:], in_=cst_d), writes=["CST"])
    PA = a32(0, 128)[0:96, :]
    PBr = a32(128, 128)[0:106, :]
    P.dma("sp", "c1", lambda h: h.dma_start(out=PA, in_=pa_d), writes=["PA"])
    P.dma("sp", "c2", lambda h: h.dma_start(out=PBr, in_=pb_d), writes=["PB"])
    GB = PRM[:, 144:176]
    P.dma("sp", "c3", lambda h: h.dma_start(out=GB, in_=gb_d.partition_broadcast(128)), writes=["GB"])
    dve(lambda h: h.tensor_copy(out=IDB, in_=IDENT), reads=["CST"], writes=["IDB"])
    dve(lambda h: h.tensor_copy(out=ONB, in_=ONES), reads=["CST"], writes=["ONB"])
    pe(lambda h: h.transpose(PS[6][:, 0:96], PA, IDENT[0:96, 0:96]), reads=["PA", "CST"], writes=[("ps", 6)])
    dve(lambda h: h.tensor_copy(out=PAT[:], in_=PS[6][:, 0:96]), reads=[("ps", 6)], writes=["PAT"])
    pe(lambda h: h.transpose(PS[7][:, 0:106], PBr, IDENT[0:106, 0:106]), reads=["PB", "CST"], writes=[("ps", 7)])
    dve(lambda h: h.tensor_copy(out=PBT[:], in_=PS[7][:, 0:106]), reads=[("ps", 7)], writes=["PBT"])
    SC = a16(0, 32, "p (k r) -> p k r", r=2)
    act(lambda h: h.activation(out=SC[:, :, 0], in_=PAT[:, 0:16], func=AF.Silu), reads=["PAT"], writes=["SC0"])
    act(lambda h: h.activation(out=SC[:, :, 1], in_=PAT[:, 16:32], func=AF.Silu), reads=["PAT"], writes=["SC1"])
    NB = 36
    wmv = wmod_d.rearrange("(k p) n -> p k n", p=128)
    WM = [WB[:, i * 8192:(i + 1) * 8192].rearrange("p (k n) -> p k n", k=KD) for i in range(3)]
    BM = [a32(512 + i * 512, 512)[0:1, :] for i in range(3)]
    MR = [a32(2048 + i * 512, 512)[0:2, :] for i in range(2)]
    for nb in range(NB):
        b3 = nb % 3
        P.dma("pool", f"wm{b3}", (lambda h, nb=nb, b3=b3: h.dma_start(out=WM[b3], in_=wmv[:, :, nb * 512:(nb + 1) * 512])),
              writes=[("wm", b3)])
        P.dma("sp", f"bm{b3}", (lambda h, nb=nb, b3=b3: h.dma_start(out=BM[b3], in_=bmod_d[0:1, nb * 512:(nb + 1) * 512])),
              writes=[("bm", b3)])
        pb = PS[nb % 2]
        for k in range(KD):
            pe((lambda h, k=k, pb=pb, b3=b3: h.matmul(pb[0:2, 0:512], SC[:, k, :], WM[b3][:, k, :], start=(k == 0), stop=False)),
               reads=[("wm", b3), "SC0", "SC1"], writes=[("ps", nb % 2)], signal=False)
        pe((lambda h, pb=pb, b3=b3: h.matmul(pb[0:2, 0:512], ONES[0:1, 0:2], BM[b3], start=False, stop=True)),
           reads=[("bm", b3), "CST"], writes=[("ps", nb % 2)])
        mr = MR[nb % 2]
        dve((lambda h, pb=pb, mr=mr: h.tensor_copy(out=mr, in_=pb[0:2, 0:512])), reads=[("ps", nb % 2)], writes=[("mr", nb % 2)])
        for c2 in range(4):
            j = nb * 4 + c2
            pe((lambda h, mr=mr, c2=c2, j=j: h.transpose(PS[2][:, 2 * j:2 * j + 2], mr[:, c2 * 128:(c2 + 1) * 128], IDENT[0:2, 0:2])),
               reads=[("mr", nb % 2), "CST"], writes=[("ps", 2)])
    dve(lambda h: h.tensor_copy(out=MODT[:], in_=PS[2][:, 0:288]), reads=[("ps", 2)], writes=["MODT"])
    if dbg:
        P.dma("sp", "dbg0", lambda h: h.dma_start(out=mods, in_=MODT[:]), reads=["MODT"])
    GS = {}
    GATE = {}
    off = 0
    for (s, r) in [(0, 0), (0, 1), (1, 0), (1, 1), (2, 0)]:
        v = PRM[:, off:off + 16]
        off += 16
        dve((lambda h, v=v, s=s, r=r: h.scalar_tensor_tensor(out=v, in0=MODv[:, 3 * s + 1, :, r], scalar=1.0,
                                                           in1=PAT[:, 32 + 16 * s:48 + 16 * s], op0=ALU.add, op1=ALU.mult)),
            reads=["MODT", "PAT"], writes=[("prm", off)])
        GS[(s, r)] = v
    for (s, r, sc) in [(0, 0, 0.5), (0, 1, 0.5), (1, 0, 1.0), (2, 0, 0.5)]:
        v = PRM[:, off:off + 16]
        off += 16
        dve((lambda h, v=v, s=s, r=r, sc=sc: h.tensor_scalar(out=v, in0=MODv[:, 3 * s + 2, :, r], scalar1=sc, scalar2=None,
                                                            op0=ALU.mult)),
            reads=["MODT"], writes=[("prm", off)])
        GATE[(s, r)] = v
    SHIFT = lambda s, r, k: MODv[:, 3 * s, k:k + 1, r]
    P.barrier()

    XT = a32(0, 8192, "p (k n) -> p k n", k=KD)
    XIN = a32(8192, 2048)
    SQ = [a32(10240 + i * 512, 512) for i in range(2)]
    TMP = [a32(11264 + i * 512, 512) for i in range(2)]
    HT = a16(0, 8192, "p (k n) -> p k n", k=KD)
    ACTT = [a16(8192 + i * 2048, 2048, "p (k n) -> p k n", k=4) for i in range(2)]
    RSTD = a16(12288, 1024).bitcast(F32)
    WU = [[WB[:, (2 * i + j) * 4096:(2 * i + j + 1) * 4096].rearrange("p (k n) -> p k n", k=KD) for j in range(2)] for i in range(2)]
    WD = [WB[:, 16384 + i * 4096:16384 + (i + 1) * 4096].rearrange("p (k n) -> p k n", k=4) for i in range(2)]
    ustate = {"n": 0}

    def stats_rstd(xt, n, dim, dst, tag):
        nk = xt.shape[1]
        for k in range(nk):
            sq = SQ[k % 2].bitcast(BF16)
            act((lambda h, sq=sq, k=k: h.activation(out=sq[:, 0:n], in_=xt[:, k, 0:n], func=AF.Square)),
                reads=[(tag, k)], writes=[("sq", k % 2)])
            pe((lambda h, sq=sq, k=k: h.matmul(PS[6][:, 0:n], ONB, sq[:, 0:n], start=(k == 0), stop=(k == nk - 1))),
               reads=[("sq", k % 2), "CST"], writes=[("ps", 6)], signal=True)
        dve((lambda h: h.tensor_scalar(out=dst[:, 0:n], in0=PS[6][:, 0:n], scalar1=1.0 / dim, scalar2=EPS, op0=ALU.mult, op1=ALU.add)),
            reads=[("ps", 6)], writes=["rstd"])
        act((lambda h: h.activation(out=dst[:, 0:n], in_=dst[:, 0:n], func=AF.Sqrt)), reads=["rstd"], writes=["rstd"])
        dve((lambda h: h.reciprocal(out=dst[:, 0:n], in_=dst[:, 0:n])), reads=["rstd"], writes=["rstd"])

    def adaln(n, s, r, dst, dtag, XT=XT, xn="xt"):
        stats_rstd(XT, n, float(D), RSTD, xn)
        gs = GS[(s, r)]
        for k in range(KD):
            tmp = TMP[k % 2]
            dve((lambda h, tmp=tmp, k=k: h.scalar_tensor_tensor(out=tmp[:, 0:n], in0=XT[:, k, 0:n], scalar=gs[:, k:k + 1],
                                                               in1=RSTD[:, 0:n], op0=ALU.mult, op1=ALU.mult)),
                reads=[(xn, k), "rstd"], writes=[("tmp", k % 2)])
            act((lambda h, tmp=tmp, k=k: h.activation(out=dst(k), in_=tmp[:, 0:n], func=AF.Identity, bias=SHIFT(s, r, k), scale=1.0)),
                reads=[("tmp", k % 2)], writes=[(dtag, k)])

    def wload(dst, src, tag):
        P.dma("pool", "w_" + str(tag[0]) + str(tag[1]), (lambda h: h.dma_start(out=dst, in_=src)), writes=[tag])

    def ffn(n, wu_d, wd_d, gate, XT=XT, HT=HT, xn="xt", hn="ht", hook=None):
        wuv = wu_d.rearrange("(k p) n -> p k n", p=128)
        wdv = wd_d.rearrange("(k p) n -> p k n", p=128)

        def up(g):
            ab = ACTT[g % 2]
            for blk in range(2):
                ui = ustate["n"] % 2
                ustate["n"] += 1
                c0 = g * 512 + blk * 256
                wload(WU[ui][0], wuv[:, :, c0:c0 + 256], ("wu", 2 * ui))
                wload(WU[ui][1], wuv[:, :, FF + c0:FF + c0 + 256], ("wu", 2 * ui + 1))
                for i in range(2):
                    par = (blk * 2 + i) % 2
                    pa_, pb_ = PS[par], PS[2 + par]
                    for k in range(KD):
                        pe((lambda h, k=k, i=i, ui=ui, pa_=pa_: h.matmul(pa_[:, 0:n], WU[ui][0][:, k, i * 128:(i + 1) * 128], HT[:, k, 0:n],
                                                                      start=(k == 0), stop=(k == KD - 1))),
                           reads=[("wu", 2 * ui), (hn, k)], writes=[("ps", par)], signal=(k == KD - 1))
                    for k in range(KD):
                        pe((lambda h, k=k, i=i, ui=ui, pb_=pb_: h.matmul(pb_[:, 0:n], WU[ui][1][:, k, i * 128:(i + 1) * 128], HT[:, k, 0:n],
                                                                      start=(k == 0), stop=(k == KD - 1))),
                           reads=[("wu", 2 * ui + 1), (hn, k)], writes=[("ps", 2 + par)], signal=(k == KD - 1))
                    sil = TMP[par]
                    act((lambda h, sil=sil, pa_=pa_: h.activation(out=sil[:, 0:n], in_=pa_[:, 0:n], func=AF.Silu)),
                        reads=[("ps", par)], writes=[("tmp", par)])
                    j = blk * 2 + i
                    dve((lambda h, sil=sil, pb_=pb_, ab=ab, j=j: h.tensor_tensor(out=ab[:, j, 0:n], in0=pb_[:, 0:n], in1=sil[:, 0:n], op=ALU.mult)),
                        reads=[("ps", 2 + par), ("tmp", par)], writes=[("act", g % 2, j)])

        def down(g):
            ab = ACTT[g % 2]
            for half in range(2):
                di = (2 * g + half) % 2
                wload(WD[di], wdv[:, 4 * g:4 * g + 4, half * 1024:(half + 1) * 1024], ("wd", di))
                for mm in range(8):
                    m = half * 8 + mm
                    py = PS[4 + (m % 4)]
                    for kk in range(4):
                        pe((lambda h, kk=kk, mm=mm, di=di, py=py: h.matmul(py[:, 0:n], WD[di][:, kk, mm * 128:(mm + 1) * 128], ab[:, kk, 0:n],
                                                                         start=(kk == 0), stop=(kk == 3))),
                           reads=[("wd", di), ("act", g % 2, kk)], writes=[("ps", 4 + (m % 4))], signal=(kk == 3))
                    dve((lambda h, m=m, py=py: h.scalar_tensor_tensor(out=XT[:, m, 0:n], in0=py[:, 0:n], scalar=gate[:, m:m + 1],
                                                                     in1=XT[:, m, 0:n], op0=ALU.mult, op1=ALU.add)),
                        reads=[("ps", 4 + (m % 4)), (xn, m)], writes=[(xn, m)])

        NG = KF // 4
        up(0)
        for g in range(NG):
            if g + 1 < NG:
                up(g + 1)
            down(g)
            if g == 0 and hook is not None:
                hook()

    def load_tokens(src_rows, n):
        for s in range(n // 128):
            P.dma("sp", "xin", (lambda h, s=s: h.dma_start(out=XIN, in_=src_rows[s * 128:(s + 1) * 128, :])), writes=["xin"])
            for q in range(4):
                pb = PS[6 + (q % 2)]
                for kk in range(4):
                    k = q * 4 + kk
                    pe((lambda h, pb=pb, k=k, kk=kk: h.transpose(pb[:, kk * 128:(kk + 1) * 128], XIN[:, k * 128:(k + 1) * 128], IDENT)),
                       reads=["xin", "CST"], writes=[("ps", 6 + (q % 2))], signal=(kk == 3))
                dst = XT[:, q * 4:q * 4 + 4, s * 128:(s + 1) * 128]
                srcv = pb[:, 0:512].rearrange("p (k n) -> p k n", k=4)
                wr = [("xt", q * 4 + kk) for kk in range(4)]
                if q % 2 == 0:
                    act((lambda h, dst=dst, srcv=srcv: h.copy(out=dst, in_=srcv)), reads=[("ps", 6 + (q % 2))], writes=wr)
                else:
                    dve((lambda h, dst=dst, srcv=srcv: h.tensor_copy(out=dst, in_=srcv)), reads=[("ps", 6 + (q % 2))], writes=wr)

    for ti, (t0, n) in enumerate(TILES):
        r = 1 if ti == 0 else 0
        rows = ctx_d if ti == 0 else x_d[t0 - 256:t0 - 256 + n, :]
        load_tokens(rows, n)
        adaln(n, 0, r, (lambda k, n=n: HT[:, k, 0:n]), "ht")
        ffn(n, w1u_d, w1d_d, GATE[(0, r)])
        if ti > 0:
            l0 = t0 - 256
            P.dma("sp", "x1st", (lambda h, l0=l0, n=n: h.dma_start(out=x1s[:, :, l0:l0 + n].rearrange("k p n -> p k n"), in_=XT[:, :, 0:n])),
                  reads=[("xt", k) for k in range(KD)], writes=["x1s"])
        adaln(n, 1, r, (lambda k, t0=t0, n=n: H2v[:, k, t0:t0 + n]), "h2")
    P.barrier()
    if dbg:
        P.dma("sp", "dbg1", lambda h: h.dma_start(out=h2s.rearrange("k p n -> p k n"), in_=H2v), reads=[])
        P.barrier()

    if stop_after == "A":
        return finish(nc, P, st, out_d, None)

    GTv = GT[:].rearrange("p (c g) -> p c g", g=32)
    GT4 = GT[:].rearrange("p (c j m) -> p c j m", j=4, m=8)
    WG = WB[:, 0:512].rearrange("p (k n) -> p k n", k=KD)
    winv = win_d.rearrange("(k p) n -> p k n", p=128)
    wload(WG, winv[:, :, 5632:5664], ("wg", 0))
    for c in range(NT):
        pb = PS[c % 2]
        for k in range(KD):
            pe((lambda h, pb=pb, c=c, k=k: h.matmul(pb[:, 0:32], H2v[:, k, c * 128:(c + 1) * 128], WG[:, k, :], start=(k == 0), stop=(k == KD - 1))),
               reads=[("wg", 0)], writes=[("ps", c % 2)], signal=(k == KD - 1))
        dve((lambda h, pb=pb, c=c: h.tensor_tensor(out=GTv[:, c, :], in0=pb[:, 0:32], in1=GB, op=ALU.add)),
            reads=[("ps", c % 2), "GB"], writes=["GT"])
    LFm = a32(0, 288, "p (c j m) -> p c j m", j=2, m=8)
    GTlf = GT[:].rearrange("p (c j2 j1 m) -> p c j2 j1 m", j2=2, j1=2, m=8)[:, :, :, 1, :]
    GTig = GT[:].rearrange("p (c j2 j1 m) -> p c j2 j1 m", j2=2, j1=2, m=8)[:, :, :, 0, :]
    act(lambda h: h.activation(out=LFm, in_=GTlf, func=AF.Exp, scale=-1.0), reads=["GT"], writes=["LFm"])
    act(lambda h: h.activation(out=LFm, in_=LFm, func=AF.Ln, bias=1.0, scale=1.0), reads=["LFm"], writes=["LFm"])
    act(lambda h: h.mul(out=LFm, in_=LFm, mul=-1.0), reads=["LFm"], writes=["LFm"])
    GQv = a32(11136, 1152, "p (d q c m) -> p d q c m", d=2, q=4, m=8)
    CSCALE = 128.0 ** -0.5
    for d_ in range(2):
        mask = MASKF if d_ == 0 else MASKB
        pe((lambda h, d_=d_, mask=mask: h.matmul(PS[2][:, 0:144], mask, LFm[:, :, d_, :], start=True, stop=True)),
           reads=["LFm", "CST"], writes=[("ps", 2)])
        pe((lambda h, d_=d_: h.matmul(PS[3][:, 0:144], ONES, LFm[:, :, d_, :], start=True, stop=True)),
           reads=["LFm", "CST"], writes=[("ps", 3)])
        Bv = PS[2][:, 0:144].rearrange("p (c m) -> p c m", m=8)
        Tv = PS[3][:, 0:144].rearrange("p (c m) -> p c m", m=8)
        ESq, ENBq, WGq, DECq = (GQv[:, d_, q] for q in range(4))
        tmpq = a32(512 + d_ * 144, 144, "p (c m) -> p c m", m=8)
        dve((lambda h, d_=d_, tmpq=tmpq, Bv=Bv: h.tensor_tensor(out=tmpq, in0=GTig[:, :, d_, :], in1=Bv, op=ALU.subtract)),
            reads=["GT", ("ps", 2)], writes=[("tq", d_)])
        act((lambda h, ESq=ESq, tmpq=tmpq: h.activation(out=ESq, in_=tmpq, func=AF.Exp)), reads=[("tq", d_)], writes=[("gq", d_, 0)])
        act((lambda h, ENBq=ENBq, Bv=Bv: h.activation(out=ENBq, in_=Bv, func=AF.Exp, scale=-1.0)), reads=[("ps", 2)], writes=[("gq", d_, 1)])
        act((lambda h, DECq=DECq, Tv=Tv: h.activation(out=DECq, in_=Tv, func=AF.Exp)), reads=[("ps", 3)], writes=[("gq", d_, 3)])
        dve((lambda h, WGq=WGq, ESq=ESq, DECq=DECq: h.scalar_tensor_tensor(out=WGq, in0=ESq, scalar=CSCALE, in1=DECq, op0=ALU.mult, op1=ALU.mult)),
            reads=[("gq", d_, 0), ("gq", d_, 3)], writes=[("gq", d_, 2)])
        dve((lambda h, ESq=ESq: h.tensor_scalar(out=ESq, in0=ESq, scalar1=CSCALE, scalar2=None, op0=ALU.mult)),
            reads=[("gq", d_, 0), ("gq", d_, 2)], writes=[("gq", d_, 0)])
    P.barrier()

    nr_state = {"j": 0, "pend": None}

    def normrope(n, gain_col, rope_pos, dst, CT, ST, NRB, wblk, col0, t0):
        par = nr_state["j"] % 2
        nr_state["j"] += 1
        ps, p1, p2 = PS[par], PS[2 + par], PS[4 + par]
        i0, i1, i2 = ("ps", par), ("ps", 2 + par), ("ps", 4 + par)
        sqb, rsb, qnb, t1b = NRB[par]
        tg = lambda x: (x, par)
        for k in range(KD):
            pe((lambda h, k=k: h.matmul(ps[:, 0:n], wblk[:, k, col0:col0 + 128], H2v[:, k, t0:t0 + n], start=(k == 0), stop=(k == KD - 1))),
               reads=["wblk"], writes=[i0], signal=(k == KD - 1))
        sqh = sqb.bitcast(BF16)
        act((lambda h: h.activation(out=sqh[:, 0:n], in_=ps[:, 0:n], func=AF.Square)), reads=[i0], writes=[tg("nr_sq")])
        pe((lambda h: h.matmul(p1[:, 0:n], ONB, sqh[:, 0:n], start=True, stop=True)), reads=[tg("nr_sq"), "CST"], writes=[i1])
        dve((lambda h: h.tensor_scalar(out=rsb[:, 0:n], in0=p1[:, 0:n], scalar1=1.0 / 128, scalar2=EPS, op0=ALU.mult, op1=ALU.add)),
            reads=[i1], writes=[tg("nr_rs")])
        act((lambda h: h.activation(out=rsb[:, 0:n], in_=rsb[:, 0:n], func=AF.Sqrt)), reads=[tg("nr_rs")], writes=[tg("nr_rs")])
        dve((lambda h: h.reciprocal(out=rsb[:, 0:n], in_=rsb[:, 0:n])), reads=[tg("nr_rs")], writes=[tg("nr_rs")])
        dve((lambda h: h.scalar_tensor_tensor(out=qnb[:, 0:n], in0=ps[:, 0:n], scalar=gain_col, in1=rsb[:, 0:n], op0=ALU.mult, op1=ALU.mult)),
            reads=[i0, tg("nr_rs")], writes=[tg("nr_qn")])

        def stage_b():
            if rope_pos is None:
                act((lambda h: h.copy(out=dst, in_=qnb[:, 0:n])), reads=[tg("nr_qn")], writes=["nr_dst"])
                return
            pe((lambda h: h.matmul(p2[:, 0:n], RM, qnb[:, 0:n], start=True, stop=True)), reads=[tg("nr_qn"), "CST"], writes=[i2])
            pool((lambda h: h.tensor_tensor(out=t1b[:, 0:n], in0=qnb[:, 0:n], in1=CT[:, rope_pos:rope_pos + n], op=ALU.mult)),
                 reads=[tg("nr_qn")], writes=[tg("nr_t1")])
            dve((lambda h: h.tensor_tensor(out=qnb[:, 0:n], in0=p2[:, 0:n], in1=ST[:, rope_pos:rope_pos + n], op=ALU.mult)),
                reads=[i2, tg("nr_qn")], writes=[tg("nr_qn")])
            dve((lambda h: h.tensor_tensor(out=dst, in0=qnb[:, 0:n], in1=t1b[:, 0:n], op=ALU.add)),
                reads=[tg("nr_qn"), tg("nr_t1")], writes=["nr_dst"])

        prev = nr_state["pend"]
        nr_state["pend"] = stage_b
        if prev is not None:
            prev()

    def normrope_flush():
        if nr_state["pend"] is not None:
            nr_state["pend"]()
            nr_state["pend"] = None

    def proj_tm(wblk, col0, c, ps, tag):
        for k in range(KD):
            pe((lambda h, k=k: h.matmul(ps[:, 0:128], H2v[:, k, c * 128:(c + 1) * 128], wblk[:, k, col0:col0 + 128], start=(k == 0), stop=(k == KD - 1))),
               reads=["wblk"], writes=[tag], signal=(k == KD - 1))

    CT = a32(0, 2048)
    ST = a32(2048, 2048)
    NRB = [[a32(4096 + j * 2048 + i * 512, 512) for i in range(4)] for j in range(2)]
    rcb = a32(8192, 512)
    QT = a16(0, 8192, "p (hh n) -> p hh n", hh=4)
    KT = a16(8192, 2304)
    VV = a16(10496, 2304, "p (c d) -> p c d", d=128)
    PT = [WB[:, 12288 + i * 512:12288 + (i + 1) * 512] for i in range(4)]
    MS = a16(13824, 512, "p (hh n) -> p hh n", hh=4)
    P.dma("sp", "rope", lambda h: h.dma_start(out=A32[:, 0:4096], in_=rope_d), writes=["rope"])
    P.barrier()
    SCALE = 128.0 ** -0.5
    for g in range(2):
        WQ = WB[:, 0:8192].rearrange("p (k n) -> p k n", k=KD)
        WK = WB[:, 8192:10240].rearrange("p (k n) -> p k n", k=KD)
        WV = WB[:, 10240:12288].rearrange("p (k n) -> p k n", k=KD)
        wload(WQ, winv[:, :, g * 512:(g + 1) * 512], ("wa", 0))
        wload(WK, winv[:, :, 1024 + g * 128:1024 + (g + 1) * 128], ("wa", 1))
        wload(WV, winv[:, :, 1280 + g * 128:1280 + (g + 1) * 128], ("wa", 2))
        P.barrier()
        for ti, (t0, n) in enumerate(TILES):
            normrope(n, PBT[:, 105:106], None if ti == 0 else t0 - 256, KT[:, t0:t0 + n], CT, ST, NRB, WK, 0, t0)
        for hh in range(4):
            for lt in range(4):
                normrope(512, PBT[:, 104:105], lt * 512, QT[:, hh, lt * 512:(lt + 1) * 512], CT, ST, NRB, WQ, hh * 128, 256 + lt * 512)
        normrope_flush()
        for c in range(NT):
            proj_tm(WV, 0, c, PS[6 + c % 2], ("ps", 6 + c % 2))
            act((lambda h, c=c: h.copy(out=VV[:, c, :], in_=PS[6 + c % 2][:, 0:128])), reads=[("ps", 6 + c % 2)], writes=["VV"])
        P.barrier()
        steps = [(qt, c) for qt in range(16) for c in range(NT)]
        AHEAD = 3

        def s_mm(si):
            qt, c = steps[si]
            pss = PS[si % 4]
            pe((lambda h, pss=pss, c=c, qt=qt: h.matmul(pss[:, 0:512], KT[:, c * 128:(c + 1) * 128], QT[:, :, qt * 128:(qt + 1) * 128],
                                                     start=True, stop=True)), writes=[("ps", si % 4)])

        for si in range(AHEAD):
            s_mm(si)
        for si, (qt, c) in enumerate(steps):
            po, pd = PS[4 + (qt % 2)], PS[6 + (qt % 2)]
            pss = PS[si % 4]
            pt = PT[si % 4]
            if si + AHEAD < len(steps):
                s_mm(si + AHEAD)
            act((lambda h, pss=pss, pt=pt: h.activation(out=pt, in_=pss[:, 0:512], func=AF.Exp, scale=SCALE)),
                reads=[("ps", si % 4)], writes=[("pt", si % 4)])
            pe((lambda h, po=po, pt=pt, c=c: h.matmul(po[:, 0:512], VV[:, c, :], pt, start=(c == 0), stop=(c == NT - 1))),
               reads=[("pt", si % 4)], writes=[("ps", 4 + (qt % 2))], signal=False)
            pe((lambda h, pd=pd, pt=pt, c=c: h.matmul(pd[:, 0:512], ONB, pt, start=(c == 0), stop=(c == NT - 1))),
               reads=[("pt", si % 4)], writes=[("ps", 6 + (qt % 2))], signal=True)
            if c == NT - 1:
                dve((lambda h, pd=pd: h.reciprocal(out=rcb, in_=pd[:, 0:512])), reads=[("ps", 6 + (qt % 2))], writes=["rcb"])
                dve((lambda h, po=po: h.tensor_tensor(out=MS, in0=po[:, 0:512].rearrange("p (hh n) -> p hh n", hh=4),
                                                     in1=rcb.rearrange("p (hh n) -> p hh n", hh=4), op=ALU.mult)),
                    reads=[("ps", 4 + (qt % 2)), "rcb"], writes=["MS"])
                P.dma("sp", "mso", (lambda h, g=g, qt=qt: h.dma_start(out=mixs[4 * g:4 * g + 4, :, qt * 128:(qt + 1) * 128].rearrange("hh p n -> p hh n"),
                                                                   in_=MS)), reads=["MS"], writes=["mixs"])
        P.barrier()

    RAWS = [a32(0, 2304), a32(2304, 2304)]
    ACC = a32(4608, 2304)
    TMPC = a32(6912, 2048)
    MOT = WB[:, 20232:22280]
    HM = a32(8960, 2048, "p (c d) -> p c d", d=128)
    CA = [GT[:, i * 130:(i + 1) * 130] for i in range(2)]
    SS = GT[:, 260:276]
    R1 = [GT[:, 276 + 2 * i:277 + 2 * i] for i in range(2)]
    HN = [GT[:, 280 + i * 128:408 + i * 128] for i in range(2)]
    DEN = GT[:, 536:568].rearrange("p (d c) -> p d c", d=2)
    DEN2 = PRM[:, 144:176].rearrange("p (d c) -> p d c", d=2)
    HMB = a32(6912, 2048, "p (c d) -> p c d", d=128)
    MQTs = [a16(0, 2304), a16(4608, 2304)]
    MKTs = [a16(2304, 2304), a16(6912, 2304)]
    KTOK = a16(9216, 2304, "p (c d) -> p c d", d=128)
    VA = a16(11520, 2340, "p (c d) -> p c d", d=130)
    CAB = [[WB[:, 22280 + (3 * i + j) * 130:22280 + (3 * i + j + 1) * 130] for j in range(3)] for i in range(2)]
    W4s = [[WB[:, s_ * 8192 + i * 2048:s_ * 8192 + (i + 1) * 2048].rearrange("p (k n) -> p k n", k=KD) for i in range(4)] for s_ in range(2)]
    VT = WB[:, 16384:18688]
    SP_ = [WB[:, 18688 + i * 128:18688 + (i + 1) * 128] for i in range(4)]
    VW = [WB[:, 19200 + i * 130:19200 + (i + 1) * 130] for i in range(4)]
    MS2 = WB[:, 19720:20232]
    SEGS = [(0, 256), (256, 2304)]
    pool(lambda h: h.memset(VA[:, :, 128:130], 1.0), writes=["VAone"])

    def load_w4(m):
        s_ = m % 2
        for i, c0 in enumerate((1536, 2560, 3584, 4608)):
            wload(W4s[s_][i], winv[:, :, c0 + m * 128:c0 + (m + 1) * 128], ("w4", s_ * 4 + i))

    def qk_proj(m):
        s_ = m % 2
        j = 0
        for wi in range(2):
            RAW = RAWS[wi]
            for ti, (t0, n) in enumerate(TILES):
                ps = PS[6 + j % 2]
                pid = ("ps", 6 + j % 2)
                j += 1
                for k in range(KD):
                    pe((lambda h, k=k, ps=ps, wi=wi, t0=t0, n=n: h.matmul(ps[:, 0:n], W4s[s_][wi][:, k, :], H2v[:, k, t0:t0 + n], start=(k == 0), stop=(k == KD - 1))),
                       reads=[("w4", s_ * 4 + wi)], writes=[pid], signal=(k == KD - 1))
                act((lambda h, ps=ps, RAW=RAW, t0=t0, n=n: h.copy(out=RAW[:, t0:t0 + n], in_=ps[:, 0:n])), reads=[pid], writes=[("raw", wi)])

    def qk_conv(m):
        s_ = m % 2
        for (wi, ch, dstT, dtag) in ((0, m, MQTs[s_], ("mq", s_)), (1, 8 + m, MKTs[s_], ("mk", s_))):
            RAW = RAWS[wi]
            rtag = ("raw", wi)
            for (a, b) in SEGS:
                dve((lambda h, RAW=RAW, a=a, b=b, ch=ch: h.tensor_scalar(out=ACC[:, a:b], in0=RAW[:, a:b], scalar1=PBT[:, 2 * 16 + ch:2 * 16 + ch + 1],
                                                                         scalar2=None, op0=ALU.mult)), reads=[rtag], writes=["acc"])
                for j in (0, 1, 3, 4):
                    sh = j - 2
                    lo = a + max(0, -sh)
                    hi = b - max(0, sh)
                    dve((lambda h, RAW=RAW, lo=lo, hi=hi, sh=sh, j=j, ch=ch: h.scalar_tensor_tensor(
                        out=ACC[:, lo:hi], in0=RAW[:, lo + sh:hi + sh], scalar=PBT[:, j * 16 + ch:j * 16 + ch + 1], in1=ACC[:, lo:hi],
                        op0=ALU.mult, op1=ALU.add)), reads=[rtag, "acc"], writes=["acc"])
            act((lambda h, dstT=dstT, ch=ch: h.activation(out=dstT, in_=ACC, func=AF.Silu, bias=PBT[:, 80 + ch:81 + ch], scale=1.0)),
                reads=["acc"], writes=[dtag])

    load_w4(0)
    for m in range(8):
        s_ = m % 2
        MQT, MKT = MQTs[s_], MKTs[s_]
        mqid, mkid = ("mq", s_), ("mk", s_)
        W4 = W4s[s_]
        if m + 1 < 8:
            load_w4(m + 1)
        nxt = []
        qk_proj(m)
        for ti, (t0, n) in enumerate(TILES):
            ps = PS[2 + ti % 2]
            for k in range(KD):
                pe((lambda h, k=k, ps=ps, t0=t0, n=n, W4=W4: h.matmul(ps[:, 0:n], W4[2][:, k, :], H2v[:, k, t0:t0 + n], start=(k == 0), stop=(k == KD - 1))),
                   reads=[("w4", s_ * 4 + 2)], writes=[("ps", 2 + ti % 2)], signal=(k == KD - 1))
            act((lambda h, ps=ps, t0=t0, n=n: h.copy(out=VT[:, t0:t0 + n], in_=ps[:, 0:n])), reads=[("ps", 2 + ti % 2)], writes=["vt"])
        for c in range(NT):
            pb16 = PS[4 + c % 2].bitcast(BF16)
            pe((lambda h, pb16=pb16, c=c: h.transpose(pb16[:, 0:128], VT[:, c * 128:(c + 1) * 128], IDB)),
               reads=["vt", "IDB"], writes=[("ps", 4 + c % 2)])
            act((lambda h, pb16=pb16, c=c: h.copy(out=VA[:, c, 0:128], in_=pb16[:, 0:128])), reads=[("ps", 4 + c % 2)], writes=["va"])
        for lt in range(4):
            ps = PS[lt % 2]
            for k in range(KD):
                pe((lambda h, k=k, ps=ps, lt=lt, W4=W4: h.matmul(ps[:, 0:512], W4[3][:, k, :], H2v[:, k, 256 + lt * 512:256 + (lt + 1) * 512], start=(k == 0), stop=(k == KD - 1))),
                   reads=[("w4", s_ * 4 + 3)], writes=[("ps", lt % 2)], signal=(k == KD - 1))
            act((lambda h, ps=ps, lt=lt: h.activation(out=MOT[:, lt * 512:(lt + 1) * 512], in_=ps[:, 0:512], func=AF.Sigmoid)), reads=[("ps", lt % 2)], writes=["mo"])
        qk_conv(m)
        for c in range(NT):
            pb16 = PS[c % 2].bitcast(BF16)
            pe((lambda h, pb16=pb16, c=c, MKT=MKT: h.transpose(pb16[:, 0:128], MKT[:, c * 128:(c + 1) * 128], IDB)),
               reads=[mkid, "IDB"], writes=[("ps", c % 2)])
            act((lambda h, pb16=pb16, c=c: h.copy(out=KTOK[:, c, :], in_=pb16[:, 0:128])), reads=[("ps", c % 2)], writes=["ktok"])
        orders = [list(range(NT)), [1, 0] + list(range(NT - 1, 1, -1))]
        for d_ in range(2):
            dve((lambda h, d_=d_: h.memset(CA[d_], 0.0)), writes=[("ca", d_)])

        def step_ctx(idx):
            cs = [orders[d_][idx] for d_ in range(2)]
            cols = [(lambda q, d_=d_, c=cs[d_], m=m: GQv[:, d_, q, c, m:m + 1]) for d_ in range(2)]
            base = 2 if idx % 2 == 0 else 6
            psos = [PS[base + d_] for d_ in range(2)]
            oid = [("ps", base + d_) for d_ in range(2)]
            return cs, cols, psos, oid

        def part1(idx):
            cs, cols, psos, oid = step_ctx(idx)
            for d_ in range(2):
                c = cs[d_]
                pss = PS[d_]
                sp = SP_[2 * d_ + idx % 2]
                mask = MASKF if d_ == 0 else MASKB
                pe((lambda h, pss=pss, c=c, MKT=MKT, MQT=MQT: h.matmul(pss[:, 0:128], MKT[:, c * 128:(c + 1) * 128], MQT[:, c * 128:(c + 1) * 128], start=True, stop=True)),
                   reads=[mqid, mkid], writes=[("ps", d_)])
                dve((lambda h, pss=pss, sp=sp, col=cols[d_], mask=mask: h.scalar_tensor_tensor(out=sp, in0=pss[:, 0:128], scalar=col(0), in1=mask,
                                                                                            op0=ALU.mult, op1=ALU.mult)),
                    reads=[("ps", d_)], writes=[("sp", d_, idx % 2)])
            for d_ in range(2):
                c = cs[d_]
                sp = SP_[2 * d_ + idx % 2]
                pe((lambda h, pso=psos[d_], sp=sp, c=c, idx=idx: h.matmul(pso[:, 0:129], sp, VA[:, c, 0:129], start=True, stop=(idx == 0))),
                   reads=[("sp", d_, idx % 2), "va", "VAone"], writes=[oid[d_]], signal=(idx == 0))
            if idx < NT - 1:
                for d_ in range(2):
                    c = cs[d_]
                    vw = VW[2 * d_ + idx % 2]
                    psu = PS[4 + d_]
                    act((lambda h, vw=vw, c=c, col=cols[d_]: h.activation(out=vw, in_=VA[:, c, :], func=AF.Copy, scale=col(2))),
                        reads=["VAone", "va"], writes=[("vw", d_, idx % 2)])
                    pe((lambda h, psu=psu, vw=vw, c=c: h.matmul(psu[:, 0:130], KTOK[:, c, :], vw, start=True, stop=True)),
                       reads=[("vw", d_, idx % 2), "ktok"], writes=[("ps", 4 + d_)])
                for d_ in range(2):
                    psu = PS[4 + d_]
                    dve((lambda h, psu=psu, d_=d_, col=cols[d_]: h.scalar_tensor_tensor(out=CA[d_], in0=CA[d_], scalar=col(3), in1=psu[:, 0:130],
                                                                                      op0=ALU.mult, op1=ALU.add)),
                        reads=[("ps", 4 + d_), ("ca", d_)], writes=[("ca", d_)])
                    act((lambda h, d_=d_, idx=idx: h.copy(out=CAB[d_][idx % 3], in_=CA[d_])), reads=[("ca", d_)], writes=[("cab", d_, idx % 3)])

        def part2(idx):
            cs, cols, psos, oid = step_ctx(idx)
            for d_ in range(2):
                c = cs[d_]
                pso = psos[d_]
                col = cols[d_]
                if idx > 0:
                    pe((lambda h, pso=pso, c=c, d_=d_, MQT=MQT, idx=idx: h.matmul(pso[:, 0:129], MQT[:, c * 128:(c + 1) * 128], CAB[d_][(idx - 1) % 3][:, 0:129], start=False, stop=True)),
                       reads=[("cab", d_, (idx - 1) % 3), mqid], writes=[oid[d_]])
                if c >= 2:
                    dst = HM if d_ == 0 else HMB
                    act((lambda h, pso=pso, c=c, dst=dst: h.copy(out=dst[:, c - 2, :], in_=pso[:, 0:128])),
                        reads=[oid[d_]], writes=[("hmx", d_, c)] + ([("hm", c)] if d_ == 0 else []))
                    act((lambda h, pso=pso, c=c, d_=d_: h.copy(out=DEN[:, d_, c - 2:c - 1], in_=pso[:, 128:129])),
                        reads=[oid[d_]], writes=[("den", d_)])

        part1(0)
        for idx in range(NT):
            if idx + 1 < NT:
                part1(idx + 1)
            part2(idx)
        for d_ in range(2):
            enb = GQv[:, d_, 1, 2:NT, m]
            dn = DEN[:, d_, :]
            dve((lambda h, dn=dn, enb=enb, d_=d_: h.tensor_tensor(out=DEN2[:, d_, :], in0=dn, in1=enb, op=ALU.max)), reads=[("den", d_)], writes=[("den2", d_)])
            dve((lambda h, dn=dn, d_=d_: h.scalar_tensor_tensor(out=dn, in0=dn, scalar=-1.0, in1=DEN2[:, d_, :], op0=ALU.mult, op1=ALU.max)),
                reads=[("den", d_), ("den2", d_)], writes=[("den", d_)])
            dve((lambda h, dn=dn: h.reciprocal(out=dn, in_=dn)), reads=[("den", d_)], writes=[("den", d_)])
        for lt in range(16):
            dve((lambda h, lt=lt: h.tensor_scalar(out=HM[:, lt, :], in0=HM[:, lt, :], scalar1=DEN[:, 0, lt:lt + 1], scalar2=None, op0=ALU.mult)),
                reads=[("hmx", 0, lt + 2), ("den", 0)], writes=[("hm", lt + 2)])
            dve((lambda h, lt=lt: h.scalar_tensor_tensor(out=HM[:, lt, :], in0=HMB[:, lt, :], scalar=DEN[:, 1, lt:lt + 1], in1=HM[:, lt, :],
                                                        op0=ALU.mult, op1=ALU.add)),
                reads=[("hmx", 1, lt + 2), ("den", 1), ("hm", lt + 2)], writes=[("hm", lt + 2)])
        for lt in range(16):
            act((lambda h, lt=lt: h.activation(out=HN[lt % 2], in_=HM[:, lt, :], func=AF.Square, accum_out=SS[:, lt:lt + 1])),
                reads=[("hm", lt + 2)], writes=[("hn", lt % 2), "ss"])
        dve((lambda h: h.tensor_scalar(out=SS, in0=SS, scalar1=1.0 / 128, scalar2=EPS, op0=ALU.mult, op1=ALU.add)), reads=["ss"], writes=["ss"])
        act((lambda h: h.activation(out=SS, in_=SS, func=AF.Sqrt)), reads=["ss"], writes=["ss"])
        dve((lambda h: h.reciprocal(out=SS, in_=SS)), reads=["ss"], writes=["ss"])
        for q4 in range(4):
            pb = PS[q4 % 2]
            for i in range(4):
                lt = q4 * 4 + i
                hn = HN[lt % 2]
                dve((lambda h, hn=hn, lt=lt: h.tensor_scalar(out=hn, in0=HM[:, lt, :], scalar1=SS[:, lt:lt + 1], scalar2=None, op0=ALU.mult)),
                    reads=[("hm", lt + 2), "ss"], writes=[("hn", lt % 2)])
                pe((lambda h, pb=pb, hn=hn, i=i: h.transpose(pb[:, i * 128:(i + 1) * 128], hn, IDENT)),
                   reads=[("hn", lt % 2), "CST"], writes=[("ps", q4 % 2)])
            dve((lambda h, pb=pb, m=m, q4=q4: h.scalar_tensor_tensor(out=MS2, in0=pb[:, 0:512], scalar=PBT[:, 96 + m:97 + m],
                                                                    in1=MOT[:, q4 * 512:(q4 + 1) * 512], op0=ALU.mult, op1=ALU.mult)),
                reads=[("ps", q4 % 2), "mo"], writes=["ms2"])
            P.dma("sp", "mso2", (lambda h, q4=q4, m=m: h.dma_start(out=mixs[8 + m, :, q4 * 512:(q4 + 1) * 512], in_=MS2)), reads=["ms2"], writes=["mixs"])
    P.barrier()

    if stop_after == "C":
        return finish(nc, P, st, out_d, None)

    MT = H2[:, 0:8192].rearrange("p (k n) -> p k n", k=KD)
    XT2 = H2[:, 8192:24576].bitcast(F32).rearrange("p (k n) -> p k n", k=KD)
    HT2 = H2[:, 24576:32768].rearrange("p (k n) -> p k n", k=KD)
    XTS, HTS, XNS, HNS = [XT, XT2], [HT, HT2], ["xt", "xt2"], ["ht", "ht2"]
    OUTS = XIN
    woutv = wout_d.rearrange("(k p) n -> p k n", p=128)
    out_events = []
    gate5 = GATE[(1, 0)]

    def prologue(lt):
        l0 = lt * 512
        XTc, HTc, xn, hn = XTS[lt % 2], HTS[lt % 2], XNS[lt % 2], HNS[lt % 2]
        P.dma("sp", "mtl", (lambda h: h.dma_start(out=MT, in_=mixs[:, :, l0:l0 + 512].rearrange("k p n -> p k n"))), writes=["mt"])
        P.dma("sp", "x1l%d" % (lt % 2), (lambda h: h.dma_start(out=XTc, in_=x1s[:, :, l0:l0 + 512].rearrange("k p n -> p k n"))),
              writes=[(xn, k) for k in range(KD)])
        for blk in range(4):
            WO = WB[:, (blk % 2) * 8192:(blk % 2 + 1) * 8192].rearrange("p (k n) -> p k n", k=KD)
            wtags = [("wu", 2 * (blk % 2)), ("wu", 2 * (blk % 2) + 1)]
            P.dma("pool", "w_wo%d" % (blk % 2), (lambda h, WO=WO, blk=blk: h.dma_start(out=WO, in_=woutv[:, :, blk * 512:(blk + 1) * 512])), writes=wtags)
            for mm in range(4):
                m = blk * 4 + mm
                py = PS[4 + (m % 2)]
                for k in range(KD):
                    pe((lambda h, k=k, mm=mm, WO=WO, py=py: h.matmul(py[:, 0:512], WO[:, k, mm * 128:(mm + 1) * 128], MT[:, k, :], start=(k == 0), stop=(k == KD - 1))),
                       reads=wtags + ["mt"], writes=[("ps", 4 + (m % 2))], signal=(k == KD - 1))
                dve((lambda h, m=m, py=py: h.scalar_tensor_tensor(out=XTc[:, m, :], in0=py[:, 0:512], scalar=gate5[:, m:m + 1], in1=XTc[:, m, :],
                                                                 op0=ALU.mult, op1=ALU.add)),
                    reads=[("ps", 4 + (m % 2)), (xn, m)], writes=[(xn, m)])
        adaln(512, 2, 0, (lambda k: HTc[:, k, 0:512]), hn, XT=XTc, xn=xn)

    def epilogue(lt):
        l0 = lt * 512
        XTc, xn = XTS[lt % 2], XNS[lt % 2]
        stats_rstd(XTc, 512, float(D), RSTD, xn)
        for k in range(KD):
            dve((lambda h, k=k: h.scalar_tensor_tensor(out=XTc[:, k, :], in0=XTc[:, k, :], scalar=PAT[:, 80 + k:81 + k], in1=RSTD[:, 0:512],
                                                      op0=ALU.mult, op1=ALU.mult)),
                reads=[(xn, k), "rstd"], writes=[(xn, k)])
        for s_ in range(4):
            for q in range(4):
                pb = PS[6 + (q % 2)]
                for kk in range(4):
                    k = q * 4 + kk
                    pe((lambda h, pb=pb, k=k, kk=kk, s_=s_: h.transpose(pb[:, kk * 128:(kk + 1) * 128], XTc[:, k, s_ * 128:(s_ + 1) * 128], IDENT)),
                       reads=[(xn, k), "CST"], writes=[("ps", 6 + (q % 2))], signal=(kk == 3))
                if q % 2 == 0:
                    act((lambda h, pb=pb, q=q: h.copy(out=OUTS[:, q * 512:(q + 1) * 512], in_=pb[:, 0:512])), reads=[("ps", 6 + (q % 2))], writes=[("outs", q)])
                else:
                    dve((lambda h, pb=pb, q=q: h.tensor_copy(out=OUTS[:, q * 512:(q + 1) * 512], in_=pb[:, 0:512])), reads=[("ps", 6 + (q % 2))], writes=[("outs", q)])
            ev = P.dma("sp", "outd", (lambda h, s_=s_: h.dma_start(out=out_d[l0 + s_ * 128:l0 + (s_ + 1) * 128, :], in_=OUTS)),
                       reads=[("outs", q) for q in range(4)], writes=["outd"])
            out_events.append(ev)

    prologue(0)
    for lt in range(4):
        hook = (lambda lt=lt: epilogue(lt - 1)) if lt > 0 else None
        ffn(512, w2u_d, w2d_d, GATE[(2, 0)], XT=XTS[lt % 2], HT=HTS[lt % 2], xn=XNS[lt % 2], hn=HNS[lt % 2], hook=hook)
        if lt + 1 < 4:
            prologue(lt + 1)
    epilogue(3)
    return finish(nc, P, st, out_d, out_events)


def finish(nc, P, st, out_d, out_events):
    P.barrier()
    P.emit()
    st.close()
    return nc


def _consts():
    ident = np.eye(128, dtype=np.float32)
    u = np.arange(128)
    maskf = (u[:, None] <= u[None, :]).astype(np.float32)
    maskb = (u[:, None] >= u[None, :]).astype(np.float32)
    ones = np.ones((128, 128), np.float32)
    rm = np.zeros((128, 128), np.float32)
    for base in (0, 64):
        for i in range(32):
            rm[base + 32 + i, base + i] = -1.0
            rm[base + i, base + 32 + i] = 1.0
    cst = np.concatenate([ident, maskf, maskb, ones, rm], axis=1).astype(np.float32)
    t = np.arange(NLAT)
    row = (t // 64).astype(np.float32)
    colp = (t % 64).astype(np.float32)
    inv = np.power(np.float32(10000.0), -np.arange(0, 64, 2, dtype=np.float32) / np.float32(64)).astype(np.float32)
    ct = np.zeros((128, NLAT), np.float32)
    stb = np.zeros((128, NLAT), np.float32)
    for dd in range(128):
        pos = row if dd < 64 else colp
        ang = (pos * inv[dd % 32]).astype(np.float32)
        ct[dd] = np.cos(ang)
        stb[dd] = np.sin(ang)
    rope = np.concatenate([ct, stb], axis=1).astype(np.float32)
    return cst, rope


_NC_CACHE = {}


def _in_maps(inputs):
    f = lambda k: np.ascontiguousarray(np.asarray(inputs[k], dtype=np.float32))
    x, c, ctx, c_ctx = f("x"), f("c"), f("ctx"), f("c_ctx")
    cst, rope = _consts()
    g_norm = f("g_norm")[0]
    pb = np.concatenate([f("conv_w")[0].reshape(80, 128), f("conv_b")[0].reshape(16, 128), f("m_gain")[0].reshape(8, 128),
                         f("q_gain")[0].reshape(1, 128), f("k_gain")[0].reshape(1, 128)], axis=0)
    shared = {
        "pb": np.ascontiguousarray(pb), "cst": cst, "rope": rope,
        "w_mod": f("w_mod")[0], "b_mod": f("b_mod")[0].reshape(1, -1),
        "w_ffn1_up": f("w_ffn1_up")[0], "w_ffn1_down": f("w_ffn1_down")[0],
        "w_ffn2_up": f("w_ffn2_up")[0], "w_ffn2_down": f("w_ffn2_down")[0],
        "w_in": f("w_in")[0], "w_out": f("w_out")[0], "gate_b": f("gate_b")[0].reshape(1, 32),
    }
    maps = []
    for b in range(x.shape[0]):
        pa = np.concatenate([c[b].reshape(16, 128), c_ctx.reshape(16, 128), g_norm.reshape(48, 128), f("g_final").reshape(16, 128)], axis=0)
        m = dict(shared)
        m.update({"x": x[b], "ctx": ctx[b], "pa": np.ascontiguousarray(pa)})
        maps.append(m)
    return maps


def kernel(**inputs):
    maps = _in_maps(inputs)
    if "nc" not in _NC_CACHE:
        _NC_CACHE["nc"] = build_nc()
    nc = _NC_CACHE["nc"]
    res = run_bass_kernel_spmd(nc, maps, core_ids=list(range(len(maps))))
    return np.stack([np.asarray(r["out"], dtype=np.float32) for r in res.results], axis=0)
```

```python
import contextlib
import numpy as np
import concourse.bass as bass
import concourse.mybir as mybir
from concourse.bass_utils import run_bass_kernel_spmd

F32 = mybir.dt.float32
BF16 = mybir.dt.bfloat16
AF = mybir.ActivationFunctionType
ALU = mybir.AluOpType

D = 2048
KD = 16
FF = 5632
KF = 44
NLAT = 2048
NCTX = 256
NTOK = NLAT + NCTX
NT = NTOK // 128
EPS = 1e-6
TILES = [(0, 256)] + [(256 + 512 * i, 512) for i in range(4)]
ENGS = ("pe", "act", "dve", "pool", "sp")


class Prog:
    def __init__(self, nc):
        self.nc = nc
        self.ops = {e: [] for e in ENGS}
        self.cnt = {e: 0 for e in ENGS}
        self.unsig = {e: False for e in ENGS}
        self.seen = {e: {} for e in ENGS}
        self.lastw = {}
        self.readers = {}
        self.dma_cnt = {}
        self.same_wait = {"pe": False, "act": True, "dve": True, "pool": True, "sp": False}

    def _deps(self, eng, reads, writes):
        deps = []
        for r in reads:
            ev = self.lastw.get(r)
            if ev is not None:
                deps.append((ev, "raw"))
        for w in writes:
            ev = self.lastw.get(w)
            if ev is not None:
                deps.append((ev, "waw"))
            for ev in self.readers.get(w, ()):
                deps.append((ev, "war"))
        waits = {}
        for (key, val), kind in deps:
            if key == eng and not self.same_wait[eng]:
                continue
            if self.seen[eng].get(key, 0) >= val:
                continue
            if waits.get(key, 0) < val:
                waits[key] = val
        for k, v in waits.items():
            self.seen[eng][k] = v
        return list(waits.items())

    def _record(self, ev, reads, writes):
        for r in reads:
            self.readers.setdefault(r, []).append(ev)
        for w in writes:
            self.lastw[w] = ev
            self.readers[w] = []

    def op(self, eng, fn, reads=(), writes=(), signal=True):
        waits = self._deps(eng, reads, writes)
        ev = (eng, self.cnt[eng] + 1)
        if signal:
            self.cnt[eng] += 1
            self.unsig[eng] = False
        else:
            self.unsig[eng] = True
        self.ops[eng].append((fn, waits, eng if signal else None, 1))
        self._record(ev, reads, writes)
        return ev

    def dma(self, queue, sem, fn, reads=(), writes=()):
        waits = self._deps(queue, reads, writes)
        self.dma_cnt[sem] = self.dma_cnt.get(sem, 0) + 16
        ev = (sem, self.dma_cnt[sem])
        self.ops[queue].append((fn, waits, sem, 16))
        self._record(ev, reads, writes)
        return ev

    def wait_all(self, eng, events):
        waits = []
        for key, val in events:
            if self.seen[eng].get(key, 0) < val:
                self.seen[eng][key] = val
                waits.append((key, val))
        self.ops[eng].append((None, waits, None, 0))

    def barrier(self):
        for e in ENGS:
            assert not self.unsig[e], e
        evs = [(e, self.cnt[e]) for e in ENGS if self.cnt[e] > 0]
        evs += [(s, c) for s, c in self.dma_cnt.items()]
        for e in ENGS:
            self.wait_all(e, [ev for ev in evs if ev[0] != e])
        self.lastw = {}
        self.readers = {}

    def emit(self):
        nc = self.nc
        for e in ENGS:
            assert not self.unsig[e], e
            assert self.cnt[e] < 60000, (e, self.cnt[e])
        semnames = list(ENGS) + sorted(self.dma_cnt.keys())
        with contextlib.ExitStack() as st:
            sems = {n: st.enter_context(nc.semaphore("s_" + n)) for n in semnames}
            block = st.enter_context(nc.Block())

            def run(handle, e):
                for fn, waits, sig, inc in self.ops[e]:
                    for k, v in waits:
                        handle.wait_ge(sems[k], v)
                    if fn is None:
                        continue
                    ins = fn(handle)
                    if sig is not None:
                        ins.then_inc(sems[sig], inc)

            @block.tensor
            def _(h):
                run(h, "pe")

            @block.scalar
            def _(h):
                run(h, "act")

            @block.vector
            def _(h):
                run(h, "dve")

            @block.gpsimd
            def _(h):
                run(h, "pool")

            @block.sync
            def _(h):
                run(h, "sp")


def build_nc(dbg=False, stop_after=None):
    nc = bass.Bass("TRN2", target_bir_lowering=False)
    din = lambda n, s, dt=F32: nc.dram_tensor(n, s, dt, kind="ExternalInput").ap()
    x_d = din("x", [NLAT, D])
    ctx_d = din("ctx", [NCTX, D])
    pa_d = din("pa", [96, 128])
    pb_d = din("pb", [106, 128])
    cst_d = din("cst", [128, 640])
    rope_d = din("rope", [128, 2 * NLAT])
    wmod_d = din("w_mod", [D, 9 * D])
    bmod_d = din("b_mod", [1, 9 * D])
    w1u_d = din("w_ffn1_up", [D, 2 * FF])
    w1d_d = din("w_ffn1_down", [FF, D])
    w2u_d = din("w_ffn2_up", [D, 2 * FF])
    w2d_d = din("w_ffn2_down", [FF, D])
    win_d = din("w_in", [D, 5664])
    wout_d = din("w_out", [D, D])
    gb_d = din("gate_b", [1, 32])
    out_d = nc.dram_tensor("out", [NLAT, D], F32, kind="ExternalOutput").ap()
    kind_s = "ExternalOutput" if dbg else "Internal"
    x1s = nc.dram_tensor("x1s", [KD, 128, NLAT], F32, kind=kind_s).ap()
    mixs = nc.dram_tensor("mixs", [KD, 128, NLAT], BF16, kind=kind_s).ap()
    if dbg:
        h2s = nc.dram_tensor("h2s", [KD, 128, NTOK], BF16, kind="ExternalOutput").ap()
        mods = nc.dram_tensor("mods", [128, 288], F32, kind="ExternalOutput").ap()

    st = contextlib.ExitStack()
    sb = lambda n, s, dt: st.enter_context(nc.sbuf_tensor(n, s, dt))
    CST = sb("CST", [128, 640], F32)
    PAT = sb("PAT", [128, 96], F32)
    PBT = sb("PBT", [128, 106], F32)
    MODT = sb("MODT", [128, 288], F32)
    PRM = sb("PRM", [128, 192], F32)
    CBF = sb("CBF", [128, 256], BF16)
    H2 = sb("H2", [128, KD * NTOK], BF16)
    WB = sb("WB", [128, 24576], BF16)
    A32 = sb("A32", [128, 12288], F32)
    A16 = sb("A16", [128, 14336], BF16)
    GT = sb("GT", [128, NT * 32], F32)
    PS = [st.enter_context(nc.psum_tensor(f"ps{i}", [128, 512], F32)) for i in range(8)]

    IDENT = CST[:, 0:128]
    MASKF = CST[:, 128:256]
    MASKB = CST[:, 256:384]
    ONES = CST[:, 384:512]
    RM = CST[:, 512:640]
    IDB = CBF[:, 0:128]
    ONB = CBF[:, 128:256]
    H2v = H2[:].rearrange("p (k n) -> p k n", k=KD)
    MODv = MODT[:].rearrange("p (i k r) -> p i k r", i=9, k=KD)
    WB32 = WB.bitcast(F32)

    P = Prog(nc)
    act = lambda fn, **kw: P.op("act", fn, **kw)
    dve = lambda fn, **kw: P.op("dve", fn, **kw)
    pool = lambda fn, **kw: P.op("pool", fn, **kw)
    pe = lambda fn, **kw: P.op("pe", fn, **kw)

    def a32(off, n, pat=None, **dims):
        v = A32[:, off:off + n]
        return v.rearrange(pat, **dims) if pat else v

    def a16(off, n, pat=None, **dims):
        v = A16[:, off:off + n]
        return v.rearrange(pat, **dims) if pat else v

    P.dma("sp", "c0", lambda h: h.dma_start(out=CST[:], in_=cst_d), writes=["CST"])
    PA = a32(0, 128)[0:96, :]
    PBr = a32(128, 128)[0:106, :]
    P.dma("sp", "c1", lambda h: h.dma_start(out=PA, in_=pa_d), writes=["PA"])
    P.dma("sp", "c2", lambda h: h.dma_start(out=PBr, in_=pb_d), writes=["PB"])
    GB = PRM[:, 144:176]
    P.dma("sp", "c3", lambda h: h.dma_start(out=GB, in_=gb_d.partition_broadcast(128)), writes=["GB"])
    dve(lambda h: h.tensor_copy(out=IDB, in_=IDENT), reads=["CST"], writes=["IDB"])
    dve(lambda h: h.tensor_copy(out=ONB, in_=ONES), reads=["CST"], writes=["ONB"])
    pe(lambda h: h.transpose(PS[6][:, 0:96], PA, IDENT[0:96, 0:96]), reads=["PA", "CST"], writes=[("ps", 6)])
    dve(lambda h: h.tensor_copy(out=PAT[:], in_=PS[6][:, 0:96]), reads=[("ps", 6)], writes=["PAT"])
    pe(lambda h: h.transpose(PS[7][:, 0:106], PBr, IDENT[0:106, 0:106]), reads=["PB", "CST"], writes=[("ps", 7)])
    dve(lambda h: h.tensor_copy(out=PBT[:], in_=PS[7][:, 0:106]), reads=[("ps", 7)], writes=["PBT"])
    SC = a16(0, 32, "p (k r) -> p k r", r=2)
    act(lambda h: h.activation(out=SC[:, :, 0], in_=PAT[:, 0:16], func=AF.Silu), reads=["PAT"], writes=["SC0"])
    act(lambda h: h.activation(out=SC[:, :, 1], in_=PAT[:, 16:32], func=AF.Silu), reads=["PAT"], writes=["SC1"])
    NB = 36
    wmv = wmod_d.rearrange("(k p) n -> p k n", p=128)
    WM = [WB[:, i * 8192:(i + 1) * 8192].rearrange("p (k n) -> p k n", k=KD) for i in range(3)]
    BM = [a32(512 + i * 512, 512)[0:1, :] for i in range(3)]
    MR = [a32(2048 + i * 512, 512)[0:2, :] for i in range(2)]
    for nb in range(NB):
        b3 = nb % 3
        P.dma("pool", f"wm{b3}", (lambda h, nb=nb, b3=b3: h.dma_start(out=WM[b3], in_=wmv[:, :, nb * 512:(nb + 1) * 512])),
              writes=[("wm", b3)])
        P.dma("sp", f"bm{b3}", (lambda h, nb=nb, b3=b3: h.dma_start(out=BM[b3], in_=bmod_d[0:1, nb * 512:(nb + 1) * 512])),
              writes=[("bm", b3)])
        pb = PS[nb % 2]
        for k in range(KD):
            pe((lambda h, k=k, pb=pb, b3=b3: h.matmul(pb[0:2, 0:512], SC[:, k, :], WM[b3][:, k, :], start=(k == 0), stop=False)),
               reads=[("wm", b3), "SC0", "SC1"], writes=[("ps", nb % 2)], signal=False)
        pe((lambda h, pb=pb, b3=b3: h.matmul(pb[0:2, 0:512], ONES[0:1, 0:2], BM[b3], start=False, stop=True)),
           reads=[("bm", b3), "CST"], writes=[("ps", nb % 2)])
        mr = MR[nb % 2]
        dve((lambda h, pb=pb, mr=mr: h.tensor_copy(out=mr, in_=pb[0:2, 0:512])), reads=[("ps", nb % 2)], writes=[("mr", nb % 2)])
        for c2 in range(4):
            j = nb * 4 + c2
            pe((lambda h, mr=mr, c2=c2, j=j: h.transpose(PS[2][:, 2 * j:2 * j + 2], mr[:, c2 * 128:(c2 + 1) * 128], IDENT[0:2, 0:2])),
               reads=[("mr", nb % 2), "CST"], writes=[("ps", 2)])
    dve(lambda h: h.tensor_copy(out=MODT[:], in_=PS[2][:, 0:288]), reads=[("ps", 2)], writes=["MODT"])
    if dbg:
        P.dma("sp", "dbg0", lambda h: h.dma_start(out=mods, in_=MODT[:]), reads=["MODT"])
    GS = {}
    GATE = {}
    off = 0
    for (s, r) in [(0, 0), (0, 1), (1, 0), (1, 1), (2, 0)]:
        v = PRM[:, off:off + 16]
        off += 16
        dve((lambda h, v=v, s=s, r=r: h.scalar_tensor_tensor(out=v, in0=MODv[:, 3 * s + 1, :, r], scalar=1.0,
                                                           in1=PAT[:, 32 + 16 * s:48 + 16 * s], op0=ALU.add, op1=ALU.mult)),
            reads=["MODT", "PAT"], writes=[("prm", off)])
        GS[(s, r)] = v
    for (s, r, sc) in [(0, 0, 0.5), (0, 1, 0.5), (1, 0, 1.0), (2, 0, 0.5)]:
        v = PRM[:, off:off + 16]
        off += 16
        dve((lambda h, v=v, s=s, r=r, sc=sc: h.tensor_scalar(out=v, in0=MODv[:, 3 * s + 2, :, r], scalar1=sc, scalar2=None,
                                                            op0=ALU.mult)),
            reads=["MODT"], writes=[("prm", off)])
        GATE[(s, r)] = v
    SHIFT = lambda s, r, k: MODv[:, 3 * s, k:k + 1, r]
    P.barrier()

    XT = a32(0, 8192, "p (k n) -> p k n", k=KD)
    XIN = a32(8192, 2048)
    SQ = [a32(10240 + i * 512, 512) for i in range(2)]
    TMP = [a32(11264 + i * 512, 512) for i in range(2)]
    HT = a16(0, 8192, "p (k n) -> p k n", k=KD)
    ACTT = [a16(8192 + i * 2048, 2048, "p (k n) -> p k n", k=4) for i in range(2)]
    RSTD = a16(12288, 1024).bitcast(F32)
    WU = [[WB[:, (2 * i + j) * 4096:(2 * i + j + 1) * 4096].rearrange("p (k n) -> p k n", k=KD) for j in range(2)] for i in range(2)]
    WD = [WB[:, 16384 + i * 4096:16384 + (i + 1) * 4096].rearrange("p (k n) -> p k n", k=4) for i in range(2)]
    ustate = {"n": 0}

    def stats_rstd(xt, n, dim, dst, tag):
        nk = xt.shape[1]
        for k in range(nk):
            sq = SQ[k % 2].bitcast(BF16)
            act((lambda h, sq=sq, k=k: h.activation(out=sq[:, 0:n], in_=xt[:, k, 0:n], func=AF.Square)),
                reads=[(tag, k)], writes=[("sq", k % 2)])
            pe((lambda h, sq=sq, k=k: h.matmul(PS[6][:, 0:n], ONB, sq[:, 0:n], start=(k == 0), stop=(k == nk - 1))),
               reads=[("sq", k % 2), "CST"], writes=[("ps", 6)], signal=True)
        dve((lambda h: h.tensor_scalar(out=dst[:, 0:n], in0=PS[6][:, 0:n], scalar1=1.0 / dim, scalar2=EPS, op0=ALU.mult, op1=ALU.add)),
            reads=[("ps", 6)], writes=["rstd"])
        act((lambda h: h.activation(out=dst[:, 0:n], in_=dst[:, 0:n], func=AF.Sqrt)), reads=["rstd"], writes=["rstd"])
        dve((lambda h: h.reciprocal(out=dst[:, 0:n], in_=dst[:, 0:n])), reads=["rstd"], writes=["rstd"])

    def adaln(n, s, r, dst, dtag, XT=XT, xn="xt"):
        stats_rstd(XT, n, float(D), RSTD, xn)
        gs = GS[(s, r)]
        for k in range(KD):
            tmp = TMP[k % 2]
            dve((lambda h, tmp=tmp, k=k: h.scalar_tensor_tensor(out=tmp[:, 0:n], in0=XT[:, k, 0:n], scalar=gs[:, k:k + 1],
                                                               in1=RSTD[:, 0:n], op0=ALU.mult, op1=ALU.mult)),
                reads=[(xn, k), "rstd"], writes=[("tmp", k % 2)])
            act((lambda h, tmp=tmp, k=k: h.activation(out=dst(k), in_=tmp[:, 0:n], func=AF.Identity, bias=SHIFT(s, r, k), scale=1.0)),
                reads=[("tmp", k % 2)], writes=[(dtag, k)])

    def wload(dst, src, tag):
        P.dma("pool", "w_" + str(tag[0]) + str(tag[1]), (lambda h: h.dma_start(out=dst, in_=src)), writes=[tag])

    def ffn(n, wu_d, wd_d, gate, XT=XT, HT=HT, xn="xt", hn="ht", hook=None):
        wuv = wu_d.rearrange("(k p) n -> p k n", p=128)
        wdv = wd_d.rearrange("(k p) n -> p k n", p=128)

        def up(g):
            ab = ACTT[g % 2]
            for blk in range(2):
                ui = ustate["n"] % 2
                ustate["n"] += 1
                c0 = g * 512 + blk * 256
                wload(WU[ui][0], wuv[:, :, c0:c0 + 256], ("wu", 2 * ui))
                wload(WU[ui][1], wuv[:, :, FF + c0:FF + c0 + 256], ("wu", 2 * ui + 1))
                for i in range(2):
                    par = (blk * 2 + i) % 2
                    pa_, pb_ = PS[par], PS[2 + par]
                    for k in range(KD):
                        pe((lambda h, k=k, i=i, ui=ui, pa_=pa_: h.matmul(pa_[:, 0:n], WU[ui][0][:, k, i * 128:(i + 1) * 128], HT[:, k, 0:n],
                                                                      start=(k == 0), stop=(k == KD - 1))),
                           reads=[("wu", 2 * ui), (hn, k)], writes=[("ps", par)], signal=(k == KD - 1))
                    for k in range(KD):
                        pe((lambda h, k=k, i=i, ui=ui, pb_=pb_: h.matmul(pb_[:, 0:n], WU[ui][1][:, k, i * 128:(i + 1) * 128], HT[:, k, 0:n],
                                                                      start=(k == 0), stop=(k == KD - 1))),
                           reads=[("wu", 2 * ui + 1), (hn, k)], writes=[("ps", 2 + par)], signal=(k == KD - 1))
                    sil = TMP[par]
                    act((lambda h, sil=sil, pa_=pa_: h.activation(out=sil[:, 0:n], in_=pa_[:, 0:n], func=AF.Silu)),
                        reads=[("ps", par)], writes=[("tmp", par)])
                    j = blk * 2 + i
                    dve((lambda h, sil=sil, pb_=pb_, ab=ab, j=j: h.tensor_tensor(out=ab[:, j, 0:n], in0=pb_[:, 0:n], in1=sil[:, 0:n], op=ALU.mult)),
                        reads=[("ps", 2 + par), ("tmp", par)], writes=[("act", g % 2, j)])

        def down(g):
            ab = ACTT[g % 2]
            for half in range(2):
                di = (2 * g + half) % 2
                wload(WD[di], wdv[:, 4 * g:4 * g + 4, half * 1024:(half + 1) * 1024], ("wd", di))
                for mm in range(8):
                    m = half * 8 + mm
                    py = PS[4 + (m % 4)]
                    for kk in range(4):
                        pe((lambda h, kk=kk, mm=mm, di=di, py=py: h.matmul(py[:, 0:n], WD[di][:, kk, mm * 128:(mm + 1) * 128], ab[:, kk, 0:n],
                                                                         start=(kk == 0), stop=(kk == 3))),
                           reads=[("wd", di), ("act", g % 2, kk)], writes=[("ps", 4 + (m % 4))], signal=(kk == 3))
                    dve((lambda h, m=m, py=py: h.scalar_tensor_tensor(out=XT[:, m, 0:n], in0=py[:, 0:n], scalar=gate[:, m:m + 1],
                                                                     in1=XT[:, m, 0:n], op0=ALU.mult, op1=ALU.add)),
                        reads=[("ps", 4 + (m % 4)), (xn, m)], writes=[(xn, m)])

        NG = KF // 4
        up(0)
        for g in range(NG):
            if g + 1 < NG:
                up(g + 1)
            down(g)
            if g == 0 and hook is not None:
                hook()

    def load_tokens(src_rows, n):
        for s in range(n // 128):
            P.dma("sp", "xin", (lambda h, s=s: h.dma_start(out=XIN, in_=src_rows[s * 128:(s + 1) * 128, :])), writes=["xin"])
            for q in range(4):
                pb = PS[6 + (q % 2)]
                for kk in range(4):
                    k = q * 4 + kk
                    pe((lambda h, pb=pb, k=k, kk=kk: h.transpose(pb[:, kk * 128:(kk + 1) * 128], XIN[:, k * 128:(k + 1) * 128], IDENT)),
                       reads=["xin", "CST"], writes=[("ps", 6 + (q % 2))], signal=(kk == 3))
                dst = XT[:, q * 4:q * 4 + 4, s * 128:(s + 1) * 128]
                srcv = pb[:, 0:512].rearrange("p (k n) -> p k n", k=4)
                wr = [("xt", q * 4 + kk) for kk in range(4)]
                if q % 2 == 0:
                    act((lambda h, dst=dst, srcv=srcv: h.copy(out=dst, in_=srcv)), reads=[("ps", 6 + (q % 2))], writes=wr)
                else:
                    dve((lambda h, dst=dst, srcv=srcv: h.tensor_copy(out=dst, in_=srcv)), reads=[("ps", 6 + (q % 2))], writes=wr)

    for ti, (t0, n) in enumerate(TILES):
        r = 1 if ti == 0 else 0
        rows = ctx_d if ti == 0 else x_d[t0 - 256:t0 - 256 + n, :]
        load_tokens(rows, n)
        adaln(n, 0, r, (lambda k, n=n: HT[:, k, 0:n]), "ht")
        ffn(n, w1u_d, w1d_d, GATE[(0, r)])
        if ti > 0:
            l0 = t0 - 256
            P.dma("sp", "x1st", (lambda h, l0=l0, n=n: h.dma_start(out=x1s[:, :, l0:l0 + n].rearrange("k p n -> p k n"), in_=XT[:, :, 0:n])),
                  reads=[("xt", k) for k in range(KD)], writes=["x1s"])
        adaln(n, 1, r, (lambda k, t0=t0, n=n: H2v[:, k, t0:t0 + n]), "h2")
    P.barrier()
    if dbg:
        P.dma("sp", "dbg1", lambda h: h.dma_start(out=h2s.rearrange("k p n -> p k n"), in_=H2v), reads=[])
        P.barrier()

    if stop_after == "A":
        return finish(nc, P, st, out_d, None)

    GTv = GT[:].rearrange("p (c g) -> p c g", g=32)
    GT4 = GT[:].rearrange("p (c j m) -> p c j m", j=4, m=8)
    WG = WB[:, 0:512].rearrange("p (k n) -> p k n", k=KD)
    winv = win_d.rearrange("(k p) n -> p k n", p=128)
    wload(WG, winv[:, :, 5632:5664], ("wg", 0))
    for c in range(NT):
        pb = PS[c % 2]
        for k in range(KD):
            pe((lambda h, pb=pb, c=c, k=k: h.matmul(pb[:, 0:32], H2v[:, k, c * 128:(c + 1) * 128], WG[:, k, :], start=(k == 0), stop=(k == KD - 1))),
               reads=[("wg", 0)], writes=[("ps", c % 2)], signal=(k == KD - 1))
        dve((lambda h, pb=pb, c=c: h.tensor_tensor(out=GTv[:, c, :], in0=pb[:, 0:32], in1=GB, op=ALU.add)),
            reads=[("ps", c % 2), "GB"], writes=["GT"])
    LFm = a32(0, 288, "p (c j m) -> p c j m", j=2, m=8)
    GTlf = GT[:].rearrange("p (c j2 j1 m) -> p c j2 j1 m", j2=2, j1=2, m=8)[:, :, :, 1, :]
    GTig = GT[:].rearrange("p (c j2 j1 m) -> p c j2 j1 m", j2=2, j1=2, m=8)[:, :, :, 0, :]
    act(lambda h: h.activation(out=LFm, in_=GTlf, func=AF.Exp, scale=-1.0), reads=["GT"], writes=["LFm"])
    act(lambda h: h.activation(out=LFm, in_=LFm, func=AF.Ln, bias=1.0, scale=1.0), reads=["LFm"], writes=["LFm"])
    act(lambda h: h.mul(out=LFm, in_=LFm, mul=-1.0), reads=["LFm"], writes=["LFm"])
    GQv = a32(11136, 1152, "p (d q c m) -> p d q c m", d=2, q=4, m=8)
    CSCALE = 128.0 ** -0.5
    for d_ in range(2):
        mask = MASKF if d_ == 0 else MASKB
        pe((lambda h, d_=d_, mask=mask: h.matmul(PS[2][:, 0:144], mask, LFm[:, :, d_, :], start=True, stop=True)),
           reads=["LFm", "CST"], writes=[("ps", 2)])
        pe((lambda h, d_=d_: h.matmul(PS[3][:, 0:144], ONES, LFm[:, :, d_, :], start=True, stop=True)),
           reads=["LFm", "CST"], writes=[("ps", 3)])
        Bv = PS[2][:, 0:144].rearrange("p (c m) -> p c m", m=8)
        Tv = PS[3][:, 0:144].rearrange("p (c m) -> p c m", m=8)
        ESq, ENBq, WGq, DECq = (GQv[:, d_, q] for q in range(4))
        tmpq = a32(512 + d_ * 144, 144, "p (c m) -> p c m", m=8)
        dve((lambda h, d_=d_, tmpq=tmpq, Bv=Bv: h.tensor_tensor(out=tmpq, in0=GTig[:, :, d_, :], in1=Bv, op=ALU.subtract)),
            reads=["GT", ("ps", 2)], writes=[("tq", d_)])
        act((lambda h, ESq=ESq, tmpq=tmpq: h.activation(out=ESq, in_=tmpq, func=AF.Exp)), reads=[("tq", d_)], writes=[("gq", d_, 0)])
        act((lambda h, ENBq=ENBq, Bv=Bv: h.activation(out=ENBq, in_=Bv, func=AF.Exp, scale=-1.0)), reads=[("ps", 2)], writes=[("gq", d_, 1)])
        act((lambda h, DECq=DECq, Tv=Tv: h.activation(out=DECq, in_=Tv, func=AF.Exp)), reads=[("ps", 3)], writes=[("gq", d_, 3)])
        dve((lambda h, WGq=WGq, ESq=ESq, DECq=DECq: h.scalar_tensor_tensor(out=WGq, in0=ESq, scalar=CSCALE, in1=DECq, op0=ALU.mult, op1=ALU.mult)),
            reads=[("gq", d_, 0), ("gq", d_, 3)], writes=[("gq", d_, 2)])
        dve((lambda h, ESq=ESq: h.tensor_scalar(out=ESq, in0=ESq, scalar1=CSCALE, scalar2=None, op0=ALU.mult)),
            reads=[("gq", d_, 0), ("gq", d_, 2)], writes=[("gq", d_, 0)])
    P.barrier()

    nr_state = {"j": 0, "pend": None}

    def normrope(n, gain_col, rope_pos, dst, CT, ST, NRB, wblk, col0, t0):
        par = nr_state["j"] % 2
        nr_state["j"] += 1
        ps, p1, p2 = PS[par], PS[2 + par], PS[4 + par]
        i0, i1, i2 = ("ps", par), ("ps", 2 + par), ("ps", 4 + par)
        sqb, rsb, qnb, t1b = NRB[par]
        tg = lambda x: (x, par)
        for k in range(KD):
            pe((lambda h, k=k: h.matmul(ps[:, 0:n], wblk[:, k, col0:col0 + 128], H2v[:, k, t0:t0 + n], start=(k == 0), stop=(k == KD - 1))),
               reads=["wblk"], writes=[i0], signal=(k == KD - 1))
        sqh = sqb.bitcast(BF16)
        act((lambda h: h.activation(out=sqh[:, 0:n], in_=ps[:, 0:n], func=AF.Square)), reads=[i0], writes=[tg("nr_sq")])
        pe((lambda h: h.matmul(p1[:, 0:n], ONB, sqh[:, 0:n], start=True, stop=True)), reads=[tg("nr_sq"), "CST"], writes=[i1])
        dve((lambda h: h.tensor_scalar(out=rsb[:, 0:n], in0=p1[:, 0:n], scalar1=1.0 / 128, scalar2=EPS, op0=ALU.mult, op1=ALU.add)),
            reads=[i1], writes=[tg("nr_rs")])
        act((lambda h: h.activation(out=rsb[:, 0:n], in_=rsb[:, 0:n], func=AF.Sqrt)), reads=[tg("nr_rs")], writes=[tg("nr_rs")])
        dve((lambda h: h.reciprocal(out=rsb[:, 0:n], in_=rsb[:, 0:n])), reads=[tg("nr_rs")], writes=[tg("nr_rs")])
        dve((lambda h: h.scalar_tensor_tensor(out=qnb[:, 0:n], in0=ps[:, 0:n], scalar=gain_col, in1=rsb[:, 0:n], op0=ALU.mult, op1=ALU.mult)),
            reads=[i0, tg("nr_rs")], writes=[tg("nr_qn")])

        def stage_b():
            if rope_pos is None:
                act((lambda h: h.copy(out=dst, in_=qnb[:, 0:n])), reads=[tg("nr_qn")], writes=["nr_dst"])
                return
            pe((lambda h: h.matmul(p2[:, 0:n], RM, qnb[:, 0:n], start=True, stop=True)), reads=[tg("nr_qn"), "CST"], writes=[i2])
            pool((lambda h: h.tensor_tensor(out=t1b[:, 0:n], in0=qnb[:, 0:n], in1=CT[:, rope_pos:rope_pos + n], op=ALU.mult)),
                 reads=[tg("nr_qn")], writes=[tg("nr_t1")])
            dve((lambda h: h.tensor_tensor(out=qnb[:, 0:n], in0=p2[:, 0:n], in1=ST[:, rope_pos:rope_pos + n], op=ALU.mult)),
                reads=[i2, tg("nr_qn")], writes=[tg("nr_qn")])
            dve((lambda h: h.tensor_tensor(out=dst, in0=qnb[:, 0:n], in1=t1b[:, 0:n], op=ALU.add)),
                reads=[tg("nr_qn"), tg("nr_t1")], writes=["nr_dst"])

        prev = nr_state["pend"]
        nr_state["pend"] = stage_b
        if prev is not None:
            prev()

    def normrope_flush():
        if nr_state["pend"] is not None:
            nr_state["pend"]()
            nr_state["pend"] = None

    def proj_tm(wblk, col0, c, ps, tag):
        for k in range(KD):
            pe((lambda h, k=k: h.matmul(ps[:, 0:128], H2v[:, k, c * 128:(c + 1) * 128], wblk[:, k, col0:col0 + 128], start=(k == 0), stop=(k == KD - 1))),
               reads=["wblk"], writes=[tag], signal=(k == KD - 1))

    CT = a32(0, 2048)
    ST = a32(2048, 2048)
    NRB = [[a32(4096 + j * 2048 + i * 512, 512) for i in range(4)] for j in range(2)]
    rcb = a32(8192, 512)
    QT = a16(0, 8192, "p (hh n) -> p hh n", hh=4)
    KT = a16(8192, 2304)
    VV = a16(10496, 2304, "p (c d) -> p c d", d=128)
    PT = [WB[:, 12288 + i * 512:12288 + (i + 1) * 512] for i in range(4)]
    MS = a16(13824, 512, "p (hh n) -> p hh n", hh=4)
    P.dma("sp", "rope", lambda h: h.dma_start(out=A32[:, 0:4096], in_=rope_d), writes=["rope"])
    P.barrier()
    SCALE = 128.0 ** -0.5
    for g in range(2):
        WQ = WB[:, 0:8192].rearrange("p (k n) -> p k n", k=KD)
        WK = WB[:, 8192:10240].rearrange("p (k n) -> p k n", k=KD)
        WV = WB[:, 10240:12288].rearrange("p (k n) -> p k n", k=KD)
        wload(WQ, winv[:, :, g * 512:(g + 1) * 512], ("wa", 0))
        wload(WK, winv[:, :, 1024 + g * 128:1024 + (g + 1) * 128], ("wa", 1))
        wload(WV, winv[:, :, 1280 + g * 128:1280 + (g + 1) * 128], ("wa", 2))
        P.barrier()
        for ti, (t0, n) in enumerate(TILES):
            normrope(n, PBT[:, 105:106], None if ti == 0 else t0 - 256, KT[:, t0:t0 + n], CT, ST, NRB, WK, 0, t0)
        for hh in range(4):
            for lt in range(4):
                normrope(512, PBT[:, 104:105], lt * 512, QT[:, hh, lt * 512:(lt + 1) * 512], CT, ST, NRB, WQ, hh * 128, 256 + lt * 512)
        normrope_flush()
        for c in range(NT):
            proj_tm(WV, 0, c, PS[6 + c % 2], ("ps", 6 + c % 2))
            act((lambda h, c=c: h.copy(out=VV[:, c, :], in_=PS[6 + c % 2][:, 0:128])), reads=[("ps", 6 + c % 2)], writes=["VV"])
        P.barrier()
        steps = [(qt, c) for qt in range(16) for c in range(NT)]
        AHEAD = 3

        def s_mm(si):
            qt, c = steps[si]
            pss = PS[si % 4]
            pe((lambda h, pss=pss, c=c, qt=qt: h.matmul(pss[:, 0:512], KT[:, c * 128:(c + 1) * 128], QT[:, :, qt * 128:(qt + 1) * 128],
                                                     start=True, stop=True)), writes=[("ps", si % 4)])

        for si in range(AHEAD):
            s_mm(si)
        for si, (qt, c) in enumerate(steps):
            po, pd = PS[4 + (qt % 2)], PS[6 + (qt % 2)]
            pss = PS[si % 4]
            pt = PT[si % 4]
            if si + AHEAD < len(steps):
                s_mm(si + AHEAD)
            act((lambda h, pss=pss, pt=pt: h.activation(out=pt, in_=pss[:, 0:512], func=AF.Exp, scale=SCALE)),
                reads=[("ps", si % 4)], writes=[("pt", si % 4)])
            pe((lambda h, po=po, pt=pt, c=c: h.matmul(po[:, 0:512], VV[:, c, :], pt, start=(c == 0), stop=(c == NT - 1))),
               reads=[("pt", si % 4)], writes=[("ps", 4 + (qt % 2))], signal=False)
            pe((lambda h, pd=pd, pt=pt, c=c: h.matmul(pd[:, 0:512], ONB, pt, start=(c == 0), stop=(c == NT - 1))),
               reads=[("pt", si % 4)], writes=[("ps", 6 + (qt % 2))], signal=True)
            if c == NT - 1:
                dve((lambda h, pd=pd: h.reciprocal(out=rcb, in_=pd[:, 0:512])), reads=[("ps", 6 + (qt % 2))], writes=["rcb"])
                dve((lambda h, po=po: h.tensor_tensor(out=MS, in0=po[:, 0:512].rearrange("p (hh n) -> p hh n", hh=4),
                                                     in1=rcb.rearrange("p (hh n) -> p hh n", hh=4), op=ALU.mult)),
                    reads=[("ps", 4 + (qt % 2)), "rcb"], writes=["MS"])
                P.dma("sp", "mso", (lambda h, g=g, qt=qt: h.dma_start(out=mixs[4 * g:4 * g + 4, :, qt * 128:(qt + 1) * 128].rearrange("hh p n -> p hh n"),
                                                                   in_=MS)), reads=["MS"], writes=["mixs"])
        P.barrier()

    RAWS = [a32(0, 2304), a32(2304, 2304)]
    ACC = a32(4608, 2304)
    TMPC = a32(6912, 2048)
    MOT = WB[:, 20232:22280]
    HM = a32(8960, 2048, "p (c d) -> p c d", d=128)
    CA = [GT[:, i * 130:(i + 1) * 130] for i in range(2)]
    SS = GT[:, 260:276]
    R1 = [GT[:, 276 + 2 * i:277 + 2 * i] for i in range(2)]
    HN = [GT[:, 280 + i * 128:408 + i * 128] for i in range(2)]
    DEN = GT[:, 536:568].rearrange("p (d c) -> p d c", d=2)
    DEN2 = PRM[:, 144:176].rearrange("p (d c) -> p d c", d=2)
    HMB = a32(6912, 2048, "p (c d) -> p c d", d=128)
    MQTs = [a16(0, 2304), a16(4608, 2304)]
    MKTs = [a16(2304, 2304), a16(6912, 2304)]
    KTOK = a16(9216, 2304, "p (c d) -> p c d", d=128)
    VA = a16(11520, 2340, "p (c d) -> p c d", d=130)
    CAB = [[WB[:, 22280 + (3 * i + j) * 130:22280 + (3 * i + j + 1) * 130] for j in range(3)] for i in range(2)]
    W4s = [[WB[:, s_ * 8192 + i * 2048:s_ * 8192 + (i + 1) * 2048].rearrange("p (k n) -> p k n", k=KD) for i in range(4)] for s_ in range(2)]
    VT = WB[:, 16384:18688]
    SP_ = [WB[:, 18688 + i * 128:18688 + (i + 1) * 128] for i in range(4)]
    VW = [WB[:, 19200 + i * 130:19200 + (i + 1) * 130] for i in range(4)]
    MS2 = WB[:, 19720:20232]
    SEGS = [(0, 256), (256, 2304)]
    pool(lambda h: h.memset(VA[:, :, 128:130], 1.0), writes=["VAone"])

    def load_w4(m):
        s_ = m % 2
        for i, c0 in enumerate((1536, 2560, 3584, 4608)):
            wload(W4s[s_][i], winv[:, :, c0 + m * 128:c0 + (m + 1) * 128], ("w4", s_ * 4 + i))

    def qk_proj(m):
        s_ = m % 2
        j = 0
        for wi in (1, 0):
            RAW = RAWS[wi]
            for ti, (t0, n) in enumerate(TILES):
                ps = PS[6 + j % 2]
                pid = ("ps", 6 + j % 2)
                j += 1
                for k in range(KD):
                    pe((lambda h, k=k, ps=ps, wi=wi, t0=t0, n=n: h.matmul(ps[:, 0:n], W4s[s_][wi][:, k, :], H2v[:, k, t0:t0 + n], start=(k == 0), stop=(k == KD - 1))),
                       reads=[("w4", s_ * 4 + wi)], writes=[pid], signal=(k == KD - 1))
                act((lambda h, ps=ps, RAW=RAW, t0=t0, n=n: h.copy(out=RAW[:, t0:t0 + n], in_=ps[:, 0:n])), reads=[pid], writes=[("raw", wi)])

    def qk_conv(m):
        s_ = m % 2
        for (wi, ch, dstT, dtag) in ((1, 8 + m, MKTs[s_], ("mk", s_)), (0, m, MQTs[s_], ("mq", s_))):
            RAW = RAWS[wi]
            rtag = ("raw", wi)
            for (a, b) in SEGS:
                dve((lambda h, RAW=RAW, a=a, b=b, ch=ch: h.tensor_scalar(out=ACC[:, a:b], in0=RAW[:, a:b], scalar1=PBT[:, 2 * 16 + ch:2 * 16 + ch + 1],
                                                                         scalar2=None, op0=ALU.mult)), reads=[rtag], writes=["acc"])
                for j in (0, 1, 3, 4):
                    sh = j - 2
                    lo = a + max(0, -sh)
                    hi = b - max(0, sh)
                    dve((lambda h, RAW=RAW, lo=lo, hi=hi, sh=sh, j=j, ch=ch: h.scalar_tensor_tensor(
                        out=ACC[:, lo:hi], in0=RAW[:, lo + sh:hi + sh], scalar=PBT[:, j * 16 + ch:j * 16 + ch + 1], in1=ACC[:, lo:hi],
                        op0=ALU.mult, op1=ALU.add)), reads=[rtag, "acc"], writes=["acc"])
            act((lambda h, dstT=dstT, ch=ch: h.activation(out=dstT, in_=ACC, func=AF.Silu, bias=PBT[:, 80 + ch:81 + ch], scale=1.0)),
                reads=["acc"], writes=[dtag])

    load_w4(0)
    for m in range(8):
        s_ = m % 2
        MQT, MKT = MQTs[s_], MKTs[s_]
        mqid, mkid = ("mq", s_), ("mk", s_)
        W4 = W4s[s_]
        if m + 1 < 8:
            load_w4(m + 1)
        nxt = []
        qk_proj(m)
        for ti, (t0, n) in enumerate(TILES):
            ps = PS[2 + ti % 2]
            for k in range(KD):
                pe((lambda h, k=k, ps=ps, t0=t0, n=n, W4=W4: h.matmul(ps[:, 0:n], W4[2][:, k, :], H2v[:, k, t0:t0 + n], start=(k == 0), stop=(k == KD - 1))),
                   reads=[("w4", s_ * 4 + 2)], writes=[("ps", 2 + ti % 2)], signal=(k == KD - 1))
            act((lambda h, ps=ps, t0=t0, n=n: h.copy(out=VT[:, t0:t0 + n], in_=ps[:, 0:n])), reads=[("ps", 2 + ti % 2)], writes=["vt"])
        for c in range(NT):
            pb16 = PS[4 + c % 2].bitcast(BF16)
            pe((lambda h, pb16=pb16, c=c: h.transpose(pb16[:, 0:128], VT[:, c * 128:(c + 1) * 128], IDB)),
               reads=["vt", "IDB"], writes=[("ps", 4 + c % 2)])
            act((lambda h, pb16=pb16, c=c: h.copy(out=VA[:, c, 0:128], in_=pb16[:, 0:128])), reads=[("ps", 4 + c % 2)], writes=["va"])
        qk_conv(m)
        for c in range(NT):
            pb16 = PS[c % 2].bitcast(BF16)
            pe((lambda h, pb16=pb16, c=c, MKT=MKT: h.transpose(pb16[:, 0:128], MKT[:, c * 128:(c + 1) * 128], IDB)),
               reads=[mkid, "IDB"], writes=[("ps", c % 2)])
            act((lambda h, pb16=pb16, c=c: h.copy(out=KTOK[:, c, :], in_=pb16[:, 0:128])), reads=[("ps", c % 2)], writes=["ktok"])
        for lt in range(4):
            ps = PS[lt % 2]
            for k in range(KD):
                pe((lambda h, k=k, ps=ps, lt=lt, W4=W4: h.matmul(ps[:, 0:512], W4[3][:, k, :], H2v[:, k, 256 + lt * 512:256 + (lt + 1) * 512], start=(k == 0), stop=(k == KD - 1))),
                   reads=[("w4", s_ * 4 + 3)], writes=[("ps", lt % 2)], signal=(k == KD - 1))
            act((lambda h, ps=ps, lt=lt: h.activation(out=MOT[:, lt * 512:(lt + 1) * 512], in_=ps[:, 0:512], func=AF.Sigmoid)), reads=[("ps", lt % 2)], writes=["mo"])
        orders = [list(range(NT)), [1, 0] + list(range(NT - 1, 1, -1))]
        for d_ in range(2):
            dve((lambda h, d_=d_: h.memset(CA[d_], 0.0)), writes=[("ca", d_)])

        def step_ctx(idx):
            cs = [orders[d_][idx] for d_ in range(2)]
            cols = [(lambda q, d_=d_, c=cs[d_], m=m: GQv[:, d_, q, c, m:m + 1]) for d_ in range(2)]
            base = 2 if idx % 2 == 0 else 6
            psos = [PS[base + d_] for d_ in range(2)]
            oid = [("ps", base + d_) for d_ in range(2)]
            return cs, cols, psos, oid

        def part1(idx):
            cs, cols, psos, oid = step_ctx(idx)
            for d_ in range(2):
                c = cs[d_]
                pss = PS[d_]
                sp = SP_[2 * d_ + idx % 2]
                mask = MASKF if d_ == 0 else MASKB
                pe((lambda h, pss=pss, c=c, MKT=MKT, MQT=MQT: h.matmul(pss[:, 0:128], MKT[:, c * 128:(c + 1) * 128], MQT[:, c * 128:(c + 1) * 128], start=True, stop=True)),
                   reads=[mqid, mkid], writes=[("ps", d_)])
                dve((lambda h, pss=pss, sp=sp, col=cols[d_], mask=mask: h.scalar_tensor_tensor(out=sp, in0=pss[:, 0:128], scalar=col(0), in1=mask,
                                                                                            op0=ALU.mult, op1=ALU.mult)),
                    reads=[("ps", d_)], writes=[("sp", d_, idx % 2)])
            for d_ in range(2):
                c = cs[d_]
                sp = SP_[2 * d_ + idx % 2]
                pe((lambda h, pso=psos[d_], sp=sp, c=c, idx=idx: h.matmul(pso[:, 0:129], sp, VA[:, c, 0:129], start=True, stop=(idx == 0))),
                   reads=[("sp", d_, idx % 2), "va", "VAone"], writes=[oid[d_]], signal=(idx == 0))
            if idx < NT - 1:
                for d_ in range(2):
                    c = cs[d_]
                    vw = VW[2 * d_ + idx % 2]
                    psu = PS[4 + d_]
                    act((lambda h, vw=vw, c=c, col=cols[d_]: h.activation(out=vw, in_=VA[:, c, :], func=AF.Copy, scale=col(2))),
                        reads=["VAone", "va"], writes=[("vw", d_, idx % 2)])
                    pe((lambda h, psu=psu, vw=vw, c=c: h.matmul(psu[:, 0:130], KTOK[:, c, :], vw, start=True, stop=True)),
                       reads=[("vw", d_, idx % 2), "ktok"], writes=[("ps", 4 + d_)])
                for d_ in range(2):
                    psu = PS[4 + d_]
                    dve((lambda h, psu=psu, d_=d_, col=cols[d_]: h.scalar_tensor_tensor(out=CA[d_], in0=CA[d_], scalar=col(3), in1=psu[:, 0:130],
                                                                                      op0=ALU.mult, op1=ALU.add)),
                        reads=[("ps", 4 + d_), ("ca", d_)], writes=[("ca", d_)])
                    act((lambda h, d_=d_, idx=idx: h.copy(out=CAB[d_][idx % 3], in_=CA[d_])), reads=[("ca", d_)], writes=[("cab", d_, idx % 3)])

        def part2(idx):
            cs, cols, psos, oid = step_ctx(idx)
            for d_ in range(2):
                c = cs[d_]
                pso = psos[d_]
                col = cols[d_]
                if idx > 0:
                    pe((lambda h, pso=pso, c=c, d_=d_, MQT=MQT, idx=idx: h.matmul(pso[:, 0:129], MQT[:, c * 128:(c + 1) * 128], CAB[d_][(idx - 1) % 3][:, 0:129], start=False, stop=True)),
                       reads=[("cab", d_, (idx - 1) % 3), mqid], writes=[oid[d_]])
                if c >= 2:
                    dst = HM if d_ == 0 else HMB
                    act((lambda h, pso=pso, c=c, dst=dst: h.copy(out=dst[:, c - 2, :], in_=pso[:, 0:128])),
                        reads=[oid[d_]], writes=[("hmx", d_, c)] + ([("hm", c)] if d_ == 0 else []))
                    act((lambda h, pso=pso, c=c, d_=d_: h.copy(out=DEN[:, d_, c - 2:c - 1], in_=pso[:, 128:129])),
                        reads=[oid[d_]], writes=[("den", d_)])

        part1(0)
        for idx in range(NT):
            if idx + 1 < NT:
                part1(idx + 1)
            part2(idx)
        for d_ in range(2):
            enb = GQv[:, d_, 1, 2:NT, m]
            dn = DEN[:, d_, :]
            dve((lambda h, dn=dn, enb=enb, d_=d_: h.tensor_tensor(out=DEN2[:, d_, :], in0=dn, in1=enb, op=ALU.max)), reads=[("den", d_)], writes=[("den2", d_)])
            dve((lambda h, dn=dn, d_=d_: h.scalar_tensor_tensor(out=dn, in0=dn, scalar=-1.0, in1=DEN2[:, d_, :], op0=ALU.mult, op1=ALU.max)),
                reads=[("den", d_), ("den2", d_)], writes=[("den", d_)])
            dve((lambda h, dn=dn: h.reciprocal(out=dn, in_=dn)), reads=[("den", d_)], writes=[("den", d_)])
        for lt in range(16):
            dve((lambda h, lt=lt: h.tensor_scalar(out=HM[:, lt, :], in0=HM[:, lt, :], scalar1=DEN[:, 0, lt:lt + 1], scalar2=None, op0=ALU.mult)),
                reads=[("hmx", 0, lt + 2), ("den", 0)], writes=[("hm", lt + 2)])
            dve((lambda h, lt=lt: h.scalar_tensor_tensor(out=HM[:, lt, :], in0=HMB[:, lt, :], scalar=DEN[:, 1, lt:lt + 1], in1=HM[:, lt, :],
                                                        op0=ALU.mult, op1=ALU.add)),
                reads=[("hmx", 1, lt + 2), ("den", 1), ("hm", lt + 2)], writes=[("hm", lt + 2)])
        for lt in range(16):
            act((lambda h, lt=lt: h.activation(out=HN[lt % 2], in_=HM[:, lt, :], func=AF.Square, accum_out=SS[:, lt:lt + 1])),
                reads=[("hm", lt + 2)], writes=[("hn", lt % 2), "ss"])
        dve((lambda h: h.tensor_scalar(out=SS, in0=SS, scalar1=1.0 / 128, scalar2=EPS, op0=ALU.mult, op1=ALU.add)), reads=["ss"], writes=["ss"])
        act((lambda h: h.activation(out=SS, in_=SS, func=AF.Sqrt)), reads=["ss"], writes=["ss"])
        dve((lambda h: h.reciprocal(out=SS, in_=SS)), reads=["ss"], writes=["ss"])
        for q4 in range(4):
            pb = PS[q4 % 2]
            for i in range(4):
                lt = q4 * 4 + i
                hn = HN[lt % 2]
                dve((lambda h, hn=hn, lt=lt: h.tensor_scalar(out=hn, in0=HM[:, lt, :], scalar1=SS[:, lt:lt + 1], scalar2=None, op0=ALU.mult)),
                    reads=[("hm", lt + 2), "ss"], writes=[("hn", lt % 2)])
                pe((lambda h, pb=pb, hn=hn, i=i: h.transpose(pb[:, i * 128:(i + 1) * 128], hn, IDENT)),
                   reads=[("hn", lt % 2), "CST"], writes=[("ps", q4 % 2)])
            dve((lambda h, pb=pb, m=m, q4=q4: h.scalar_tensor_tensor(out=MS2, in0=pb[:, 0:512], scalar=PBT[:, 96 + m:97 + m],
                                                                    in1=MOT[:, q4 * 512:(q4 + 1) * 512], op0=ALU.mult, op1=ALU.mult)),
                reads=[("ps", q4 % 2), "mo"], writes=["ms2"])
            P.dma("sp", "mso2", (lambda h, q4=q4, m=m: h.dma_start(out=mixs[8 + m, :, q4 * 512:(q4 + 1) * 512], in_=MS2)), reads=["ms2"], writes=["mixs"])
    P.barrier()

    if stop_after == "C":
        return finish(nc, P, st, out_d, None)

    MT = H2[:, 0:8192].rearrange("p (k n) -> p k n", k=KD)
    XT2 = H2[:, 8192:24576].bitcast(F32).rearrange("p (k n) -> p k n", k=KD)
    HT2 = H2[:, 24576:32768].rearrange("p (k n) -> p k n", k=KD)
    XTS, HTS, XNS, HNS = [XT, XT2], [HT, HT2], ["xt", "xt2"], ["ht", "ht2"]
    OUTS = XIN
    woutv = wout_d.rearrange("(k p) n -> p k n", p=128)
    out_events = []
    gate5 = GATE[(1, 0)]

    def prologue(lt):
        l0 = lt * 512
        XTc, HTc, xn, hn = XTS[lt % 2], HTS[lt % 2], XNS[lt % 2], HNS[lt % 2]
        P.dma("sp", "mtl", (lambda h: h.dma_start(out=MT, in_=mixs[:, :, l0:l0 + 512].rearrange("k p n -> p k n"))), writes=["mt"])
        P.dma("sp", "x1l%d" % (lt % 2), (lambda h: h.dma_start(out=XTc, in_=x1s[:, :, l0:l0 + 512].rearrange("k p n -> p k n"))),
              writes=[(xn, k) for k in range(KD)])
        for blk in range(4):
            WO = WB[:, (blk % 2) * 8192:(blk % 2 + 1) * 8192].rearrange("p (k n) -> p k n", k=KD)
            wtags = [("wu", 2 * (blk % 2)), ("wu", 2 * (blk % 2) + 1)]
            P.dma("pool", "w_wo%d" % (blk % 2), (lambda h, WO=WO, blk=blk: h.dma_start(out=WO, in_=woutv[:, :, blk * 512:(blk + 1) * 512])), writes=wtags)
            for mm in range(4):
                m = blk * 4 + mm
                py = PS[4 + (m % 2)]
                for k in range(KD):
                    pe((lambda h, k=k, mm=mm, WO=WO, py=py: h.matmul(py[:, 0:512], WO[:, k, mm * 128:(mm + 1) * 128], MT[:, k, :], start=(k == 0), stop=(k == KD - 1))),
                       reads=wtags + ["mt"], writes=[("ps", 4 + (m % 2))], signal=(k == KD - 1))
                dve((lambda h, m=m, py=py: h.scalar_tensor_tensor(out=XTc[:, m, :], in0=py[:, 0:512], scalar=gate5[:, m:m + 1], in1=XTc[:, m, :],
                                                                 op0=ALU.mult, op1=ALU.add)),
                    reads=[("ps", 4 + (m % 2)), (xn, m)], writes=[(xn, m)])
        adaln(512, 2, 0, (lambda k: HTc[:, k, 0:512]), hn, XT=XTc, xn=xn)

    def epilogue(lt):
        l0 = lt * 512
        XTc, xn = XTS[lt % 2], XNS[lt % 2]
        stats_rstd(XTc, 512, float(D), RSTD, xn)
        for k in range(KD):
            dve((lambda h, k=k: h.scalar_tensor_tensor(out=XTc[:, k, :], in0=XTc[:, k, :], scalar=PAT[:, 80 + k:81 + k], in1=RSTD[:, 0:512],
                                                      op0=ALU.mult, op1=ALU.mult)),
                reads=[(xn, k), "rstd"], writes=[(xn, k)])
        for s_ in range(4):
            for q in range(4):
                pb = PS[6 + (q % 2)]
                for kk in range(4):
                    k = q * 4 + kk
                    pe((lambda h, pb=pb, k=k, kk=kk, s_=s_: h.transpose(pb[:, kk * 128:(kk + 1) * 128], XTc[:, k, s_ * 128:(s_ + 1) * 128], IDENT)),
                       reads=[(xn, k), "CST"], writes=[("ps", 6 + (q % 2))], signal=(kk == 3))
                if q % 2 == 0:
                    act((lambda h, pb=pb, q=q: h.copy(out=OUTS[:, q * 512:(q + 1) * 512], in_=pb[:, 0:512])), reads=[("ps", 6 + (q % 2))], writes=[("outs", q)])
                else:
                    dve((lambda h, pb=pb, q=q: h.tensor_copy(out=OUTS[:, q * 512:(q + 1) * 512], in_=pb[:, 0:512])), reads=[("ps", 6 + (q % 2))], writes=[("outs", q)])
            ev = P.dma("sp", "outd", (lambda h, s_=s_: h.dma_start(out=out_d[l0 + s_ * 128:l0 + (s_ + 1) * 128, :], in_=OUTS)),
                       reads=[("outs", q) for q in range(4)], writes=["outd"])
            out_events.append(ev)

    prologue(0)
    for lt in range(4):
        hook = (lambda lt=lt: epilogue(lt - 1)) if lt > 0 else None
        ffn(512, w2u_d, w2d_d, GATE[(2, 0)], XT=XTS[lt % 2], HT=HTS[lt % 2], xn=XNS[lt % 2], hn=HNS[lt % 2], hook=hook)
        if lt + 1 < 4:
            prologue(lt + 1)
    epilogue(3)
    return finish(nc, P, st, out_d, out_events)


def finish(nc, P, st, out_d, out_events):
    P.barrier()
    P.emit()
    st.close()
    return nc


def _consts():
    ident = np.eye(128, dtype=np.float32)
    u = np.arange(128)
    maskf = (u[:, None] <= u[None, :]).astype(np.float32)
    maskb = (u[:, None] >= u[None, :]).astype(np.float32)
    ones = np.ones((128, 128), np.float32)
    rm = np.zeros((128, 128), np.float32)
    for base in (0, 64):
        for i in range(32):
            rm[base + 32 + i, base + i] = -1.0
            rm[base + i, base + 32 + i] = 1.0
    cst = np.concatenate([ident, maskf, maskb, ones, rm], axis=1).astype(np.float32)
    t = np.arange(NLAT)
    row = (t // 64).astype(np.float32)
    colp = (t % 64).astype(np.float32)
    inv = np.power(np.float32(10000.0), -np.arange(0, 64, 2, dtype=np.float32) / np.float32(64)).astype(np.float32)
    ct = np.zeros((128, NLAT), np.float32)
    stb = np.zeros((128, NLAT), np.float32)
    for dd in range(128):
        pos = row if dd < 64 else colp
        ang = (pos * inv[dd % 32]).astype(np.float32)
        ct[dd] = np.cos(ang)
        stb[dd] = np.sin(ang)
    rope = np.concatenate([ct, stb], axis=1).astype(np.float32)
    return cst, rope


_NC_CACHE = {}


def _in_maps(inputs):
    f = lambda k: np.ascontiguousarray(np.asarray(inputs[k], dtype=np.float32))
    x, c, ctx, c_ctx = f("x"), f("c"), f("ctx"), f("c_ctx")
    cst, rope = _consts()
    g_norm = f("g_norm")[0]
    pb = np.concatenate([f("conv_w")[0].reshape(80, 128), f("conv_b")[0].reshape(16, 128), f("m_gain")[0].reshape(8, 128),
                         f("q_gain")[0].reshape(1, 128), f("k_gain")[0].reshape(1, 128)], axis=0)
    shared = {
        "pb": np.ascontiguousarray(pb), "cst": cst, "rope": rope,
        "w_mod": f("w_mod")[0], "b_mod": f("b_mod")[0].reshape(1, -1),
        "w_ffn1_up": f("w_ffn1_up")[0], "w_ffn1_down": f("w_ffn1_down")[0],
        "w_ffn2_up": f("w_ffn2_up")[0], "w_ffn2_down": f("w_ffn2_down")[0],
        "w_in": f("w_in")[0], "w_out": f("w_out")[0], "gate_b": f("gate_b")[0].reshape(1, 32),
    }
    maps = []
    for b in range(x.shape[0]):
        pa = np.concatenate([c[b].reshape(16, 128), c_ctx.reshape(16, 128), g_norm.reshape(48, 128), f("g_final").reshape(16, 128)], axis=0)
        m = dict(shared)
        m.update({"x": x[b], "ctx": ctx[b], "pa": np.ascontiguousarray(pa)})
        maps.append(m)
    return maps


def kernel(**inputs):
    maps = _in_maps(inputs)
    if "nc" not in _NC_CACHE:
        _NC_CACHE["nc"] = build_nc()
    nc = _NC_CACHE["nc"]
    res = run_bass_kernel_spmd(nc, maps, core_ids=list(range(len(maps))))
    return np.stack([np.asarray(r["out"], dtype=np.float32) for r in res.results], axis=0)
```
